# Optimizing a Trainium2 kernel written in Bass

```python
import jax, jax.numpy as jnp
from jax import lax
import numpy as np

D_MODEL = 1024
BATCH = 2
SEQ = 8192
DEPTH = 1

CHUNK = 64
N_META = 16
EPS = 1e-6
ATT_HEADS = 8
ATT_KV_HEADS = 2
ATT_HEAD_DIM = 64
WINDOW = 128
WINDOW_CHUNKS = WINDOW // CHUNK
GLA_HEADS = 4
GLA_DK = 64
GLA_DV = 128
GLA_GATE_RANK = 16
GLA_GATE_TEMP = 16.0
D_FF = 2816

ATT_Q = ATT_HEADS * ATT_HEAD_DIM
ATT_KV = ATT_KV_HEADS * ATT_HEAD_DIM
GLA_QK = GLA_HEADS * GLA_DK
GLA_V = GLA_HEADS * GLA_DV
SPLITS = (ATT_Q, ATT_KV, ATT_KV, GLA_QK, GLA_QK, GLA_V, GLA_V, GLA_GATE_RANK, D_MODEL, D_MODEL)
D_IN = ATT_Q + 2 * ATT_KV + 2 * GLA_QK + 2 * GLA_V + GLA_GATE_RANK + 2 * D_MODEL

kernel_name = "hybrid_swa_sink_alibi_gla_macaron_meta"


def rmsnorm(x, g):
    xf = x.astype(jnp.float32)
    y = xf * lax.rsqrt(jnp.mean(xf * xf, axis=-1, keepdims=True) + EPS)
    return (y * g.astype(jnp.float32)).astype(x.dtype)


def swiglu(x, w_gate, w_up, w_down):
    return (jax.nn.silu(x @ w_gate) * (x @ w_up)) @ w_down


def window_sink_attention(q, k, v, sinks):
    B, L = q.shape[0], q.shape[1]
    n_frames = L - N_META
    nc = n_frames // CHUNK
    G = ATT_HEADS // ATT_KV_HEADS
    NB = (WINDOW_CHUNKS + 1) * CHUNK
    pad = WINDOW_CHUNKS * CHUNK
    scale = ATT_HEAD_DIM ** -0.5
    slopes = jnp.exp2(-8.0 * jnp.arange(1, ATT_HEADS + 1, dtype=jnp.float32) / ATT_HEADS).reshape(ATT_KV_HEADS, G)
    sink = sinks.astype(jnp.float32).reshape(ATT_KV_HEADS, G)

    qm = q[:, :N_META].reshape(B, N_META, ATT_KV_HEADS, G, ATT_HEAD_DIM)
    qf = q[:, N_META:].reshape(B, nc, CHUNK, ATT_KV_HEADS, G, ATT_HEAD_DIM)
    km, kf = k[:, :N_META], k[:, N_META:]
    vm, vf = v[:, :N_META], v[:, N_META:]

    def band(t):
        tp = jnp.pad(t, ((0, 0), (pad, 0), (0, 0), (0, 0)))
        tp = tp.reshape(B, nc + WINDOW_CHUNKS, CHUNK, ATT_KV_HEADS, ATT_HEAD_DIM)
        return jnp.concatenate([tp[:, j:j + nc] for j in range(WINDOW_CHUNKS + 1)], axis=2)

    kb, vb = band(kf), band(vf)
    qi = jnp.arange(nc)[:, None] * CHUNK + jnp.arange(CHUNK)[None, :]
    ki = jnp.arange(nc)[:, None] * CHUNK - pad + jnp.arange(NB)[None, :]
    dist = jnp.abs(qi[:, :, None] - ki[:, None, :]).astype(jnp.float32)
    valid = (ki >= 0)[:, None, :]

    s_band = jnp.einsum('bcqkgd,bcskd->bkgcqs', qf, kb).astype(jnp.float32) * scale
    s_band = s_band - slopes[None, :, :, None, None, None] * dist[None, None, None]
    s_band = jnp.where(valid, s_band, -jnp.inf)
    s_meta = jnp.einsum('bcqkgd,bmkd->bkgcqm', qf, km).astype(jnp.float32) * scale
    s_sink = jnp.broadcast_to(sink[None, :, :, None, None, None], (B, ATT_KV_HEADS, G, nc, CHUNK, 1))
    p = jax.nn.softmax(jnp.concatenate([s_meta, s_band, s_sink], axis=-1), axis=-1).astype(v.dtype)
    of = (jnp.einsum('bkgcqm,bmkd->bcqkgd', p[..., :N_META], vm)
          + jnp.einsum('bkgcqs,bcskd->bcqkgd', p[..., N_META:N_META + NB], vb))
    of = of.reshape(B, n_frames, ATT_Q)

    sm = jnp.einsum('bqkgd,bmkd->bkgqm', qm, km).astype(jnp.float32) * scale
    sm_sink = jnp.broadcast_to(sink[None, :, :, None, None], (B, ATT_KV_HEADS, G, N_META, 1))
    pm = jax.nn.softmax(jnp.concatenate([sm, sm_sink], axis=-1), axis=-1).astype(v.dtype)
    om = jnp.einsum('bkgqm,bmkd->bqkgd', pm[..., :N_META], vm).reshape(B, N_META, ATT_Q)
    return jnp.concatenate([om, of], axis=1)


def gated_linear_attention(q, k, v, log_a):
    B, L = q.shape[0], q.shape[1]
    lead = (-L) % CHUNK
    nc = (L + lead) // CHUNK

    def chunked(t):
        t = jnp.pad(t.astype(jnp.float32), ((0, 0), (lead, 0), (0, 0), (0, 0)))
        return t.reshape(B, nc, CHUNK, t.shape[2], t.shape[3]).transpose(0, 3, 1, 2, 4)

    qc = chunked(q) * (GLA_DK ** -0.5)
    kc, vc, gc = chunked(k), chunked(v), chunked(log_a)
    b = jnp.cumsum(gc, axis=3)
    qe = qc * jnp.exp(b)
    ke = kc * jnp.exp(-b)
    causal = jnp.tril(jnp.ones((CHUNK, CHUNK), dtype=bool))
    att = jnp.where(causal, jnp.einsum('bhcid,bhcjd->bhcij', qe, ke), 0.0)
    o_intra = jnp.einsum('bhcij,bhcjv->bhciv', att, vc)

    b_last = b[:, :, :, -1:]
    chunk_kv = jnp.einsum('bhcjd,bhcjv->bhcdv', kc * jnp.exp(b_last - b), vc)
    decay = jnp.exp(b_last[:, :, :, 0])

    def step(S, inp):
        dec, kv = inp
        return dec[..., None] * S + kv, S

    S0 = jnp.zeros((B, q.shape[2], GLA_DK, GLA_DV), jnp.float32)
    _, S_start = lax.scan(step, S0, (jnp.moveaxis(decay, 2, 0), jnp.moveaxis(chunk_kv, 2, 0)))
    S_start = jnp.moveaxis(S_start, 0, 2)
    o_inter = jnp.einsum('bhcid,bhcdv->bhciv', qe, S_start)
    o = (o_intra + o_inter).transpose(0, 2, 3, 1, 4).reshape(B, nc * CHUNK, q.shape[2], GLA_DV)
    return o[:, lead:]


def setup_inputs(seed: int = 0) -> dict:
    key = jax.random.key(seed)
    ks = jax.random.split(key, 24)
    f32 = jnp.float32

    def nrm(k, shape, scale):
        return jax.random.normal(k, shape, f32) * scale

    def gain(k, shape):
        return 1.0 + 0.02 * jax.random.normal(k, shape, f32)

    return {
        "x": nrm(ks[0], (BATCH, SEQ, D_MODEL), 1.0),
        "meta_tokens": nrm(ks[1], (N_META, D_MODEL), 1.0),
        "ffn1_norm": gain(ks[2], (DEPTH, D_MODEL)),
        "ffn1_w_gate": nrm(ks[3], (DEPTH, D_MODEL, D_FF), D_MODEL ** -0.5),
        "ffn1_w_up": nrm(ks[4], (DEPTH, D_MODEL, D_FF), D_MODEL ** -0.5),
        "ffn1_w_down": nrm(ks[5], (DEPTH, D_FF, D_MODEL), D_FF ** -0.5),
        "mix_norm": gain(ks[6], (DEPTH, D_MODEL)),
        "w_in": nrm(ks[7], (DEPTH, D_MODEL, D_IN), D_MODEL ** -0.5),
        "gla_gate_w": nrm(ks[8], (DEPTH, GLA_GATE_RANK, GLA_QK), GLA_GATE_RANK ** -0.5),
        "gla_gate_b": nrm(ks[9], (DEPTH, GLA_QK), 0.1),
        "gla_out_norm": gain(ks[10], (DEPTH, GLA_DV)),
        "att_sinks": nrm(ks[11], (DEPTH, ATT_HEADS), 1.0),
        "w_branch_att": nrm(ks[12], (DEPTH, ATT_Q, D_MODEL), ATT_Q ** -0.5),
        "w_branch_gla": nrm(ks[13], (DEPTH, GLA_V, D_MODEL), GLA_V ** -0.5),
        "w_out": nrm(ks[14], (DEPTH, D_MODEL, D_MODEL), D_MODEL ** -0.5),
        "ffn2_norm": gain(ks[15], (DEPTH, D_MODEL)),
        "ffn2_w_gate": nrm(ks[16], (DEPTH, D_MODEL, D_FF), D_MODEL ** -0.5),
        "ffn2_w_up": nrm(ks[17], (DEPTH, D_MODEL, D_FF), D_MODEL ** -0.5),
        "ffn2_w_down": nrm(ks[18], (DEPTH, D_FF, D_MODEL), D_FF ** -0.5),
        "final_norm": gain(ks[19], (D_MODEL,)),
    }


def reference(x, meta_tokens, ffn1_norm, ffn1_w_gate, ffn1_w_up, ffn1_w_down, mix_norm, w_in,
              gla_gate_w, gla_gate_b, gla_out_norm, att_sinks, w_branch_att, w_branch_gla, w_out,
              ffn2_norm, ffn2_w_gate, ffn2_w_up, ffn2_w_down, final_norm):
    B = x.shape[0]
    meta = jnp.broadcast_to(meta_tokens[None].astype(x.dtype), (B, N_META, D_MODEL))
    h = jnp.concatenate([meta, x], axis=1)
    L = h.shape[1]
    offsets = np.cumsum(SPLITS)[:-1]
    for l in range(DEPTH):
        h = h + 0.5 * swiglu(rmsnorm(h, ffn1_norm[l]), ffn1_w_gate[l], ffn1_w_up[l], ffn1_w_down[l])

        u = rmsnorm(h, mix_norm[l])
        (aq, ak, av, gq, gk, gv, gr, g_low, gate_a, gate_b) = jnp.split(u @ w_in[l], offsets, axis=-1)

        ya = window_sink_attention(aq.reshape(B, L, ATT_HEADS, ATT_HEAD_DIM),
                                   ak.reshape(B, L, ATT_KV_HEADS, ATT_HEAD_DIM),
                                   av.reshape(B, L, ATT_KV_HEADS, ATT_HEAD_DIM),
                                   att_sinks[l])

        z = (g_low @ gla_gate_w[l] + gla_gate_b[l]).astype(jnp.float32)
        log_a = jax.nn.log_sigmoid(z) / GLA_GATE_TEMP
        og = gated_linear_attention(gq.reshape(B, L, GLA_HEADS, GLA_DK),
                                    gk.reshape(B, L, GLA_HEADS, GLA_DK),
                                    gv.reshape(B, L, GLA_HEADS, GLA_DV),
                                    log_a.reshape(B, L, GLA_HEADS, GLA_DK)).astype(x.dtype)
        og = rmsnorm(og, gla_out_norm[l]) * jax.nn.silu(gr.reshape(B, L, GLA_HEADS, GLA_DV))
        yb = og.reshape(B, L, GLA_V)

        mixed = (jax.nn.sigmoid(gate_a) * (ya @ w_branch_att[l])
                 + jax.nn.sigmoid(gate_b) * (yb @ w_branch_gla[l]))
        h = h + mixed @ w_out[l]

        h = h + 0.5 * swiglu(rmsnorm(h, ffn2_norm[l]), ffn2_w_gate[l], ffn2_w_up[l], ffn2_w_down[l])
    return rmsnorm(h, final_norm)[:, N_META:]
```

```python
import numpy as np
import concourse.bass as bass
import concourse.mybir as mybir
from concourse.bass_utils import run_bass_kernel_spmd

F32 = mybir.dt.float32
BF16 = mybir.dt.bfloat16
AF = mybir.ActivationFunctionType
ALU = mybir.AluOpType
AX = mybir.AxisListType

D = 1024
DFF = 2816
NF = DFF // 128
G = 2
NG = NF // G
T_ALL = 2304
T_OWN = 2048
OWN0 = 256
EPS = 1e-6
N_CORES = 8
ENGS = ['pe', 'act', 'dve', 'sp']

import os
STAGE = int(os.environ.get('KSTAGE', '99'))


class Prog:
    NS = 24

    def __init__(self, nc):
        self.nc = nc
        self.eng = {'pe': nc.tensor, 'act': nc.scalar, 'dve': nc.vector,
                    'pool': nc.gpsimd, 'sp': nc.sync}
        self.ops = []

    def add(self, eng, fn, r=(), w=(), dma=False):
        self.ops.append((eng, fn, tuple(r), tuple(w), dma))

    def emit(self):
        nc = self.nc
        ops = self.ops
        n = len(ops)
        last_w = {}
        readers = {}
        AENG = ENGS + ['pool']
        eng_n = {e: 0 for e in AENG}
        op_local = [0] * n
        clock = {e: {} for e in AENG}
        op_clock = [None] * n
        waits = [None] * n
        needed = set()
        dma_sem_of = {}
        dma_cnt = [0] * (self.NS + 1)
        ndma = 0
        for i, (e, fn, r, w, dma) in enumerate(ops):
            raw = set()
            deps = set()
            for k in r:
                j = last_w.get(k)
                if j is not None:
                    deps.add(j)
                    raw.add(j)
            for k in w:
                j = last_w.get(k)
                if j is not None:
                    deps.add(j)
                for j in readers.get(k, ()):
                    deps.add(j)
            deps.discard(i)
            ck = clock[e]
            li = eng_n[e]
            my_w = []
            for j in sorted(deps, reverse=True):
                ej, _, _, _, dmaj = ops[j]
                if dmaj:
                    s, v = dma_sem_of[j]
                    if ck.get(('d', s), 0) >= v:
                        continue
                    my_w.append(('d', s, v))
                else:
                    lj = op_local[j]
                    if ej == e:
                        if e == 'pe' or dma:
                            continue
                        if j not in raw or li - lj > 3:
                            continue
                        if ck.get(('self', e), -1) >= lj:
                            continue
                        my_w.append(('e', ej, j))
                        needed.add(j)
                        ck[('self', e)] = lj
                        continue
                    if ck.get(ej, -1) >= lj:
                        continue
                    my_w.append(('e', ej, j))
                    needed.add(j)
                for kk, vv in op_clock[j].items():
                    if ck.get(kk, -1) < vv:
                        ck[kk] = vv
            waits[i] = my_w
            oc = {kk: vv for kk, vv in ck.items() if not (isinstance(kk, tuple) and kk[0] == 'self')}
            if dma:
                if dma == 'cc':
                    s = self.NS
                    dma_cnt[s] += 1
                else:
                    s = ndma % self.NS
                    ndma += 1
                    dma_cnt[s] += 16
                dma_sem_of[i] = (s, dma_cnt[s])
                oc[('d', s)] = dma_cnt[s]
            else:
                op_local[i] = li
                eng_n[e] = li + 1
                oc[e] = li
            op_clock[i] = oc
            for k in r:
                readers.setdefault(k, []).append(i)
            for k in w:
                last_w[k] = i
                readers[k] = []
        del op_clock
        sems = {e: nc.alloc_semaphore(name='s_' + e) for e in AENG}
        dsems = [nc.alloc_semaphore(name='d%d' % s) for s in range(self.NS + 1)]
        sig = {e: 0 for e in AENG}
        sigval = {}
        for i, (e, fn, r, w, dma) in enumerate(ops):
            if i in needed:
                sig[e] += 1
                sigval[i] = sig[e]
        if os.environ.get('KSIM'):
            q = {e: [i for i in range(n) if ops[i][0] == e] for e in AENG}
            pos = {e: 0 for e in AENG}
            sv = {e: 0 for e in AENG}
            dv = [0] * (self.NS + 1)
            prog = True
            while prog:
                prog = False
                for e in AENG:
                    while pos[e] < len(q[e]):
                        i = q[e][pos[e]]
                        ok = True
                        for wt in waits[i]:
                            if wt[0] == 'd':
                                ok = ok and dv[wt[1]] >= wt[2]
                            else:
                                ok = ok and sv[wt[1]] >= sigval[wt[2]]
                        if not ok:
                            break
                        if ops[i][4]:
                            dv[dma_sem_of[i][0]] += (1 if ops[i][4] == 'cc' else 16)
                        elif i in needed:
                            sv[e] += 1
                        pos[e] += 1
                        prog = True
            for e in AENG:
                if pos[e] < len(q[e]):
                    i = q[e][pos[e]]
                    print("DEADLOCK", e, pos[e], len(q[e]), i, ops[i][2], ops[i][3], waits[i],
                          [(wt, sigval.get(wt[2])) for wt in waits[i] if wt[0] == 'e'], sv, dv)
            print("SIM done", pos)
        for i, (e, fn, r, w, dma) in enumerate(ops):
            E = self.eng[e]
            for wt in waits[i]:
                if wt[0] == 'd':
                    E.wait_ge(dsems[wt[1]], wt[2])
                else:
                    E.wait_ge(sems[wt[1]], sigval[wt[2]])
            step = 1 if dma == 'cc' else 16
            if dma and dma_sem_of[i][1] > step:
                E.wait_ge(dsems[dma_sem_of[i][0]], dma_sem_of[i][1] - step)
            ins = fn()
            if dma:
                ins.then_inc(dsems[dma_sem_of[i][0]], step)
            elif i in needed:
                ins.then_inc(sems[e], 1)
        self.stats = (n, {e: eng_n[e] for e in AENG}, ndma, len(needed))
        self.sigstats = (dict(sig), max(dma_cnt))


COFF = {}
_o = 0
for _n, _w in (('dist', 272), ('dist1', 272), ('ident', 128), ('U2', 128), ('SU', 128),
               ('bmask', 512), ('tmask', 512), ('sinkb', 8), ('gn', 128), ('tokm', 1), ('one', 1), ('fmeta', 1), ('sel', 8), ('nsel', 8)):
    COFF[_n] = (_o, _w)
    _o += _w
CW = _o


def supertiles(t0, t1, size=512, split_last=False):
    out = []
    t = t0
    while t < t1:
        nn = min(size, t1 - t)
        out.append((t, nn))
        t += nn
    if split_last and out[-1][1] == size:
        a, nn = out.pop()
        out.append((a, nn // 2))
        out.append((a + nn // 2, nn // 2))
    return out


def build_nc():
    nc = bass.Bass("TRN2", target_bir_lowering=False)
    P = Prog(nc)

    def din(name, shape):
        return nc.dram_tensor(name, list(shape), F32, kind="ExternalInput").ap()

    xT_d = din("xT", [128, 8, T_ALL])
    norms_d = din("norms", [128, 4, 8])
    ffn_w_d = []
    for l in (1, 2):
        ffn_w_d.append((din("f%d_wg" % l, [NG, 128, G * 1024]),
                        din("f%d_wu" % l, [NG, 128, G * 1024]),
                        din("f%d_wd" % l, [NG, 128, G * 1024])))
    out_d = nc.dram_tensor("outT", [128, 8, T_OWN], F32, kind="ExternalOutput").ap()

    from contextlib import ExitStack
    with ExitStack() as es:
        def sb(name, shape, dt):
            return es.enter_context(nc.sbuf_tensor("sb_" + name, list(shape), dt))

        def ps(name, shape, dt=F32):
            return es.enter_context(nc.psum_tensor(name, list(shape), dt))

        hT = sb("hT", [128, 8, T_ALL], F32)
        norms = sb("norms", [128, 4, 8], F32)
        ones = sb("ones", [128, 128], BF16)
        banks = [ps("bank%d" % i, [128, 512]) for i in range(7)]
        ptb = ps("bankT", [128, 1024], BF16)

        epsb = sb("epsb", [128, 1], F32)
        P.add('dve', lambda: nc.vector.memset(ones[:], 1.0), w=['ones'])
        P.add('dve', lambda: nc.vector.memset(epsb[:], EPS), w=['epsb'])
        P.add('sp', lambda: nc.sync.dma_start(out=norms[:], in_=norms_d), w=['norms'], dma=True)
        for c in range(8):
            for (t0, nn) in supertiles(0, T_ALL, 1152):
                P.add('sp', (lambda c=c, t0=t0, nn=nn: nc.sync.dma_start(
                    out=hT[:, c, t0:t0 + nn], in_=xT_d[:, c, t0:t0 + nn])),
                    w=['h%d.%d' % (c, tt) for tt in range(t0 // 128, (t0 + nn) // 128)], dma=True)

        def hkeys(c, t0, nn):
            return ['h%d.%d' % (c, tt) for tt in range(t0 // 128, (t0 + nn) // 128)]

        def ukeys_default(c, t0, nn):
            return ['u%d.%d' % (c, tt) for tt in range(t0 // 128, (t0 + nn) // 128)]

        def norm_pass(sts, gi, sq, rs, tag, uT, uoff=0, ukeys=None):
            ukeys = ukeys or ukeys_default
            cnt = [0]
            for (t0, nn) in sts:
                bS = 6
                for c in range(8):
                    q = cnt[0] % 2
                    cnt[0] += 1
                    P.add('act', (lambda c=c, q=q, t0=t0, nn=nn: nc.scalar.activation(
                        out=sq[:, q, :nn], in_=hT[:, c, t0:t0 + nn], func=AF.Square)),
                        r=hkeys(c, t0, nn), w=['sq%d' % q])
                    P.add('pe', (lambda c=c, q=q, nn=nn: nc.tensor.matmul(
                        banks[bS][:, :nn], ones[:], sq[:, q, :nn], start=(c == 0), stop=(c == 7))),
                        r=['ones', 'sq%d' % q], w=['bank%d' % bS])
                P.add('act', (lambda nn=nn: nc.scalar.activation(
                    out=rs[:, :nn], in_=banks[bS][:, :nn], func=AF.Sqrt, bias=epsb[:, 0:1],
                    scale=1.0 / D)), r=['bank%d' % bS, 'epsb'], w=['rs'])
                P.add('dve', (lambda nn=nn: nc.vector.reciprocal(
                    out=rs[:, :nn], in_=rs[:, :nn])), r=['rs'], w=['rs'])
                for c in range(8):
                    P.add('dve', (lambda c=c, t0=t0, nn=nn: nc.vector.scalar_tensor_tensor(
                        out=uT[:, c, t0 - uoff:t0 - uoff + nn], in0=hT[:, c, t0:t0 + nn],
                        scalar=norms[:, gi, c:c + 1], in1=rs[:, :nn],
                        op0=ALU.mult, op1=ALU.mult)),
                        r=hkeys(c, t0, nn) + ['rs', 'norms'], w=ukeys(c, t0, nn))

        def ffn(sts, wd3, tag, uT):
            ukeys = ukeys_default
            wg_d, wu_d, wd_d = wd3
            with ExitStack() as fs:
                def fsb(name, shape, dt):
                    return fs.enter_context(nc.sbuf_tensor(tag + name, list(shape), dt))
                stg = [fsb("stg%d" % k, [128, G * 1024], F32) for k in range(3)]
                wbf = [[fsb("wbf%d_%d" % (s, k), [128, G * 1024], BF16) for k in range(3)]
                       for s in range(2)]
                sg = fsb("sg", [128, 2, 512], F32)
                act = fsb("act", [128, 2 * G, 512], BF16)
                K = lambda s: tag + s
                gu_cnt = [0]

                def load_group(gi):
                    s = gi % 2
                    for k, wdr in enumerate((wg_d, wu_d, wd_d)):
                        P.add('sp', (lambda k=k, wdr=wdr, gi=gi: nc.sync.dma_start(
                            out=stg[k][:], in_=wdr[gi])), w=[K('stg%d' % k)], dma=True)
                        P.add('act', (lambda k=k, s=s: nc.scalar.copy(
                            out=wbf[s][k][:], in_=stg[k][:])),
                            r=[K('stg%d' % k)], w=[K('wbf%d_%d' % (s, k))])

                def gate_up(gi, sti):
                    s = gi % 2
                    t0, nn = sts[sti]
                    par = gu_cnt[0] % 2
                    gu_cnt[0] += 1
                    for g in range(G):
                        bG = g
                        bU = 2 + g
                        for k, bk in ((0, bG), (1, bU)):
                            for c in range(8):
                                P.add('pe', (lambda k=k, bk=bk, c=c, g=g, s=s, t0=t0, nn=nn: nc.tensor.matmul(
                                    banks[bk][:, :nn],
                                    wbf[s][k][:, g * 1024 + c * 128: g * 1024 + (c + 1) * 128],
                                    uT[:, c, t0:t0 + nn], start=(c == 0), stop=(c == 7))),
                                    r=[K('wbf%d_%d' % (s, k))] + ukeys(c, t0, nn), w=['bank%d' % bk])
                        P.add('act', (lambda bG=bG, g=g, nn=nn: nc.scalar.activation(
                            out=sg[:, g, :nn], in_=banks[bG][:, :nn], func=AF.Silu)),
                            r=['bank%d' % bG], w=[K('sg%d' % g)])
                        a = par * G + g
                        P.add('dve', (lambda a=a, g=g, bU=bU, nn=nn: nc.vector.tensor_tensor(
                            out=act[:, a, :nn], in0=sg[:, g, :nn], in1=banks[bU][:, :nn], op=ALU.mult)),
                            r=[K('sg%d' % g), 'bank%d' % bU], w=[K('act%d' % a)])
                    return par

                def down(gi, sti, par):
                    s = gi % 2
                    t0, nn = sts[sti]
                    for m in range(8):
                        bY = 4 + (m % 3)
                        for g in range(G):
                            a = par * G + g
                            P.add('pe', (lambda bY=bY, g=g, a=a, m=m, s=s, nn=nn: nc.tensor.matmul(
                                banks[bY][:, :nn],
                                wbf[s][2][:, g * 1024 + m * 128: g * 1024 + (m + 1) * 128],
                                act[:, a, :nn], start=(g == 0), stop=(g == G - 1))),
                                r=[K('wbf%d_2' % s), K('act%d' % a)], w=['bank%d' % bY])
                        P.add('dve', (lambda bY=bY, m=m, t0=t0, nn=nn: nc.vector.scalar_tensor_tensor(
                            out=hT[:, m, t0:t0 + nn], in0=banks[bY][:, :nn], scalar=0.5,
                            in1=hT[:, m, t0:t0 + nn], op0=ALU.mult, op1=ALU.add)),
                            r=['bank%d' % bY] + hkeys(m, t0, nn), w=hkeys(m, t0, nn))

                load_group(0)
                pend = None
                for gi in range(NG):
                    for sti in range(len(sts)):
                        par = gate_up(gi, sti)
                        if pend is not None:
                            down(*pend)
                        pend = (gi, sti, par)
                        if sti == 0 and gi + 1 < NG:
                            load_group(gi + 1)
                down(*pend)
                barrier(tag)

        bscr = sb("bscr", [128, 8], F32)
        bscr2 = sb("bscr2", [128, 8], F32)

        def tiny(e, col):
            if e == 'pe':
                return lambda: nc.tensor.matmul(banks[6][0:1, 0:1], ones[0:1, 0:1], ones[0:1, 0:1],
                                                start=True, stop=True)
            if e == 'act':
                return lambda: nc.scalar.copy(out=bscr[0:1, col:col + 1], in_=epsb[0:1, 0:1])
            if e == 'dve':
                return lambda: nc.vector.tensor_copy(out=bscr[0:1, col:col + 1], in_=epsb[0:1, 0:1])
            return lambda: nc.sync.dma_start(out=bscr2[0:1, col:col + 1], in_=epsb[0:1, 0:1])

        def barrier(tag):
            allk = ['__bar_' + e for e in ENGS]
            xr = {'pe': ['ones', 'bank6'], 'act': ['epsb'], 'dve': ['epsb'], 'sp': ['epsb']}
            xw = {'pe': ['bank6'], 'act': ['bscrA0'], 'dve': ['bscrD0'], 'sp': ['bscrS0']}
            xw2 = {'pe': ['bank6'], 'act': ['bscrA1'], 'dve': ['bscrD1'], 'sp': ['bscrS1']}
            col = {'pe': 0, 'act': 0, 'dve': 2, 'sp': 4}
            for e in ENGS:
                P.add(e, tiny(e, col[e]), r=xr[e], w=['__bar_' + e] + xw[e], dma=(e == 'sp'))
            for e in ENGS:
                P.add(e, tiny(e, col[e] + 1), r=allk + xr[e], w=xw2[e], dma=(e == 'sp'))

        SLOPES = [2.0 ** (-8.0 * (h + 1) / 8) for h in range(8)]
        cst_d = din("cst", [128, CW])
        win_d = din("win", [128, 8, 4368])
        wab_d = din("wab", [128, 8, 1024])
        wout_d = din("wo", [128, 8, 1024])
        cc_in = nc.dram_tensor("cc_in", [128, 516], F32)
        cc_out = nc.dram_tensor("cc_out", [N_CORES * 128, 516], F32)
        gw_d = din("gw", [17, 256])

        with ExitStack() as ns:
            sq = ns.enter_context(nc.sbuf_tensor("sq", [128, 2, 512], BF16))
            rs = ns.enter_context(nc.sbuf_tensor("rs", [128, 512], F32))
            cst = ns.enter_context(nc.sbuf_tensor("cstb", [128, CW], F32))
            P.add('sp', lambda: nc.sync.dma_start(out=cst[:], in_=cst_d), w=['cst'], dma=True)

            def C(name):
                o, wdt = COFF[name]
                return cst[:, o:o + wdt]

            with ExitStack() as p1:
                uT = p1.enter_context(nc.sbuf_tensor("uT1", [128, 8, T_ALL], BF16))
                if STAGE >= 1 and not os.environ.get('KSKIP1'):
                    norm_pass(supertiles(0, T_ALL), 0, sq, rs, "n1", uT)
                    ffn(supertiles(0, T_ALL), ffn_w_d[0], "f1", uT)

            if STAGE >= 3:
              with ExitStack() as pm:
                def msb(name, shape, dt):
                    return pm.enter_context(nc.sbuf_tensor("m_" + name, list(shape), dt))
                yaT = msb("yaT", [128, 4, T_OWN], BF16)
                ybT = msb("ybT", [128, 4, T_OWN], BF16)
                ident = msb("ident", [128, 128], BF16)
                gwb = msb("gwb", [16, 256], BF16)
                gbb = msb("gbb", [1, 256], BF16)
                gw32 = msb("gw32", [16, 256], F32)
                gb32 = msb("gb32", [1, 256], F32)
                P.add('dve', lambda: nc.vector.tensor_copy(out=ident[:], in_=C('ident')), r=['cst'], w=['ident'])
                P.add('sp', lambda: nc.sync.dma_start(out=gw32[:], in_=gw_d[0:16, :]), w=['gw32'], dma=True)
                P.add('sp', lambda: nc.sync.dma_start(out=gb32[:], in_=gw_d[16:17, :]), w=['gb32'], dma=True)
                P.add('dve', lambda: nc.vector.tensor_copy(out=gwb[:], in_=gw32[:]), r=['gw32'], w=['gwb'])
                P.add('dve', lambda: nc.vector.tensor_copy(out=gbb[:], in_=gb32[:]), r=['gb32'], w=['gbb'])
                rr = [0]

                def nb():
                    rr[0] = (rr[0] + 1) % 6
                    return rr[0]

                def BK(b):
                    return 'bank%s' % b

                stgw = msb("stgw", [128, 8, 128], F32)

                def load_cols(dst_fn, col0, ncols, key, src=None):
                    src = win_d if src is None else src
                    for c0 in range(0, ncols, 128):
                        w_ = min(128, ncols - c0)
                        P.add('sp', (lambda c0=c0, w_=w_: nc.sync.dma_start(
                            out=stgw[:, :, :w_], in_=src[:, :, col0 + c0:col0 + c0 + w_])),
                            w=['stgw'], dma=True)
                        for (dst, lo, hi) in dst_fn(c0, w_):
                            P.add('dve', (lambda dst=dst, lo=lo, hi=hi: nc.vector.tensor_copy(
                                out=dst, in_=stgw[:, :, lo:hi])), r=['stgw'], w=[key])

                pu = ExitStack()
                ust = pu.enter_context(nc.sbuf_tensor("m_ust", [128, 8, 512], BF16))

                def ust_keys(c, t0, nn):
                    return ['ust']

                if True:
                  with ExitStack() as pa:
                    def asb(name, shape, dt):
                        return pa.enter_context(nc.sbuf_tensor("a_" + name, list(shape), dt))
                    WQ = asb("WQ", [128, 8, 512], BF16)
                    WKd = asb("WKd", [128, 8, 2, 128], BF16)
                    WV = asb("WV", [128, 8, 128], BF16)
                    load_cols(lambda c0, w_: [(WQ[:, :, c0:c0 + w_], 0, w_)], 0, 512, 'WQ')
                    load_cols(lambda c0, w_: [(WKd[:, :, g, d0:d0 + 64], g * 64, g * 64 + 64)
                                              for g in range(2) for d0 in (0, 64)], 512, 128, 'WKd')
                    load_cols(lambda c0, w_: [(WV[:, :, :], 0, 128)], 640, 128, 'WV')
                    kbuf = asb("kbuf", [128, 3, 2, 128], BF16)
                    vbuf = asb("vbuf", [128, 3, 128], BF16)
                    qT = asb("qT", [128, 4, 128], BF16)
                    Sb = asb("Sb", [128, 2, 272], F32)
                    Pb = asb("Pb", [128, 2, 272], BF16)
                    PT = asb("PT", [128, 2, 3, 128], BF16)
                    mx = asb("mx", [128, 8], F32)
                    negm = asb("negm", [128, 8], F32)
                    rsum = asb("rsum", [128, 8], F32)
                    es = asb("es", [128, 8], F32)
                    ya = asb("ya", [128, 512], BF16)

                    def kv_tile(ul, slot):
                        for g in range(2):
                            b = nb()
                            for c in range(8):
                                P.add('pe', (lambda b=b, g=g, c=c: nc.tensor.matmul(
                                    banks[b][:, 0:128], WKd[:, c, g, :], ust[:, c, ul:ul + 128],
                                    start=(c == 0), stop=(c == 7))), r=['WKd', 'ust'], w=[BK(b)])
                            P.add('act', (lambda b=b, g=g: nc.scalar.copy(
                                out=kbuf[:, slot, g, :], in_=banks[b][:, 0:128])),
                                r=[BK(b)], w=['kbuf%d' % slot])
                        b = nb()
                        for c in range(8):
                            P.add('pe', (lambda b=b, c=c: nc.tensor.matmul(
                                banks[b][:, 0:128], ust[:, c, ul:ul + 128], WV[:, c, :],
                                start=(c == 0), stop=(c == 7))), r=['WV', 'ust'], w=[BK(b)])
                        P.add('act', (lambda b=b: nc.scalar.copy(
                            out=vbuf[:, slot, :], in_=banks[b][:, 0:128])), r=[BK(b)], w=['vbuf%d' % slot])

                    def att_tile(ul, ti, cur, prev):
                        dist = C('dist1') if ti == 0 else C('dist')
                        for ci in range(4):
                            b = nb()
                            for c in range(8):
                                P.add('pe', (lambda b=b, ci=ci, c=c: nc.tensor.matmul(
                                    banks[b][:, 0:128], WQ[:, c, ci * 128:(ci + 1) * 128],
                                    ust[:, c, ul:ul + 128], start=(c == 0), stop=(c == 7))),
                                    r=['WQ', 'ust'], w=[BK(b)])
                            P.add('act', (lambda b=b, ci=ci: nc.scalar.mul(
                                out=qT[:, ci, :], in_=banks[b][:, 0:128], mul=0.125)),
                                r=[BK(b)], w=['qT'])
                        bo = nb()
                        for h in range(8):
                            g, ci, r0, q = h // 4, h // 2, (h % 2) * 64, h % 2
                            b = nb()
                            if b == bo:
                                b = nb()
                            for (slot, c0, w_) in ((prev, 0, 128), (cur, 128, 128), (2, 256, 16)):
                                P.add('pe', (lambda b=b, slot=slot, c0=c0, w_=w_, g=g, ci=ci, r0=r0: nc.tensor.matmul(
                                    banks[b][:, c0:c0 + w_], qT[r0:r0 + 64, ci, :],
                                    kbuf[r0:r0 + 64, slot, g, 0:w_], start=True, stop=True)),
                                    r=['qT', 'kbuf%d' % slot], w=[BK(b)])
                            P.add('dve', (lambda b=b, q=q, h=h, dist=dist: nc.vector.scalar_tensor_tensor(
                                out=Sb[:, q, :], in0=dist, scalar=-SLOPES[h], in1=banks[b][:, 0:272],
                                op0=ALU.mult, op1=ALU.add)), r=[BK(b), 'cst'], w=['Sb%d' % q])
                            P.add('dve', (lambda q=q, h=h: nc.vector.tensor_reduce(
                                out=mx[:, h:h + 1], in_=Sb[:, q, :], axis=AX.X, op=ALU.max)),
                                r=['Sb%d' % q], w=['mx'])
                            P.add('dve', (lambda h=h: nc.vector.tensor_scalar(
                                out=negm[:, h:h + 1], in0=mx[:, h:h + 1], scalar1=C('sinkb')[:, h:h + 1],
                                scalar2=-1.0, op0=ALU.max, op1=ALU.mult)), r=['mx', 'cst'], w=['negm'])
                            P.add('act', (lambda q=q, h=h: nc.scalar.activation(
                                out=Pb[:, q, :], in_=Sb[:, q, :], func=AF.Exp, bias=negm[:, h:h + 1],
                                scale=1.0, accum_out=rsum[:, h:h + 1])),
                                r=['Sb%d' % q, 'negm'], w=['Pb%d' % q, 'rsum'])
                            bt = 'T'
                            ptv = ptb
                            for (blk, c0, w_) in ((0, 0, 128), (1, 128, 128), (2, 256, 16)):
                                P.add('pe', (lambda ptv=ptv, blk=blk, c0=c0, w_=w_, q=q: nc.tensor.transpose(
                                    ptv[0:w_, blk * 128:(blk + 1) * 128], Pb[:, q, c0:c0 + w_], ident[:])),
                                    r=['Pb%d' % q, 'ident'], w=[BK(bt)])
                            P.add('dve', (lambda ptv=ptv, q=q: nc.vector.tensor_copy(
                                out=PT[:, q, 0:2, :], in_=ptv[:, 0:256].rearrange("p (a b) -> p a b", a=2))),
                                r=[BK(bt)], w=['PT%d' % q])
                            P.add('dve', (lambda ptv=ptv, q=q: nc.vector.tensor_copy(
                                out=PT[0:16, q, 2, :], in_=ptv[0:16, 256:384])), r=[BK(bt)], w=['PT%d' % q])
                            for (blk, slot, kk) in ((0, prev, 128), (1, cur, 128), (2, 2, 16)):
                                P.add('pe', (lambda bo=bo, blk=blk, slot=slot, kk=kk, q=q, g=g, h=h: nc.tensor.matmul(
                                    banks[bo][:, h * 64:(h + 1) * 64], PT[0:kk, q, blk, :],
                                    vbuf[0:kk, slot, g * 64:(g + 1) * 64], start=(blk == 0), stop=(blk == 2))),
                                    r=['PT%d' % q, 'vbuf%d' % slot], w=[BK(bo)])
                        P.add('dve', lambda: nc.vector.tensor_tensor(
                            out=es[:], in0=C('sinkb'), in1=negm[:], op=ALU.add), r=['negm', 'cst'], w=['es'])
                        P.add('act', lambda: nc.scalar.activation(out=es[:], in_=es[:], func=AF.Exp),
                              r=['es'], w=['es'])
                        P.add('dve', lambda: nc.vector.tensor_tensor(
                            out=es[:], in0=es[:], in1=rsum[:], op=ALU.add), r=['es', 'rsum'], w=['es'])
                        P.add('dve', lambda: nc.vector.reciprocal(out=es[:], in_=es[:]), r=['es'], w=['es'])
                        for h in range(8):
                            P.add('dve', (lambda bo=bo, h=h: nc.vector.tensor_scalar(
                                out=ya[:, h * 64:(h + 1) * 64], in0=banks[bo][:, h * 64:(h + 1) * 64],
                                scalar1=es[:, h:h + 1], scalar2=None, op0=ALU.mult)),
                                r=[BK(bo), 'es'], w=['ya'])
                        bt = 'T'
                        ptv = ptb
                        for k4 in range(4):
                            P.add('pe', (lambda ptv=ptv, k4=k4: nc.tensor.transpose(
                                ptv[:, k4 * 128:(k4 + 1) * 128], ya[:, k4 * 128:(k4 + 1) * 128], ident[:])),
                                r=['ya', 'ident'], w=[BK(bt)])
                        P.add('act', (lambda ptv=ptv, ti=ti: nc.scalar.copy(
                            out=yaT[:, :, ti * 128:(ti + 1) * 128],
                            in_=ptv[:, 0:512].rearrange("p (a b) -> p a b", a=4))),
                            r=[BK(bt)], w=['yaT%d' % ti])

                    norm_pass([(0, 256)], 1, sq, rs, "na0", ust, 0, ust_keys)
                    kv_tile(0, 2)
                    kv_tile(128, 1)
                    for sti, (t0, nn) in enumerate(supertiles(OWN0, T_ALL)):
                        norm_pass([(t0, nn)], 1, sq, rs, "na", ust, t0, ust_keys)
                        for j in range(4):
                            ti = sti * 4 + j
                            cur, prev = ti % 2, (ti + 1) % 2
                            kv_tile(j * 128, cur)
                            att_tile(j * 128, ti, cur, prev)
                    barrier("att")

                if STAGE >= 4:
                  with ExitStack() as pg:
                    def gsb(name, shape, dt):
                        return pg.enter_context(nc.sbuf_tensor("g_" + name, list(shape), dt))
                    WGqd = gsb("WGqd", [128, 8, 4, 128], BF16)
                    WGkd = gsb("WGkd", [128, 8, 4, 128], BF16)
                    WGk = gsb("WGk", [128, 8, 256], BF16)
                    WGv = gsb("WGv", [128, 8, 512], BF16)
                    WGr = gsb("WGr", [128, 8, 512], BF16)
                    WGl = gsb("WGl", [128, 8, 16], BF16)

                    def dupdst(Wd):
                        def f(c0, w_):
                            h0 = c0 // 64
                            return [(Wd[:, :, h0 + hh, d0:d0 + 64], hh * 64, hh * 64 + 64)
                                    for hh in range(2) for d0 in (0, 64)]
                        return f
                    load_cols(dupdst(WGqd), 768, 256, 'WGqd')
                    load_cols(dupdst(WGkd), 1024, 256, 'WGkd')
                    load_cols(lambda c0, w_: [(WGk[:, :, c0:c0 + w_], 0, w_)], 1024, 256, 'WGk')
                    load_cols(lambda c0, w_: [(WGv[:, :, c0:c0 + w_], 0, w_)], 1280, 512, 'WGv')
                    load_cols(lambda c0, w_: [(WGr[:, :, c0:c0 + w_], 0, w_)], 1792, 512, 'WGr')
                    load_cols(lambda c0, w_: [(WGl[:, :, 0:16], 0, 16)], 2304, 16, 'WGl')
                    glT = gsb("glT", [16, 128], BF16)
                    ez = gsb("ez", [128, 256], F32)
                    la = gsb("la", [128, 256], F32)
                    lad = gsb("lad", [128, 4, 128], F32)
                    ee1 = gsb("ee1", [128, 256], F32)
                    dec = gsb("dec", [128, 4, 2], F32)
                    kdd = gsb("kdd", [128, 4, 128], BF16)
                    vg = gsb("vg", [128, 512], BF16)
                    S = gsb("S", [128, 512], F32)
                    Y = gsb("Y", [128, 512], BF16)
                    eb = gsb("eb", [128, 512], F32)
                    enb = gsb("enb", [128, 512], F32)
                    tq = gsb("tq", [128, 512], F32)
                    X = gsb("X", [128, 512], BF16)
                    keT = gsb("keT", [128, 512], BF16)
                    aTm = gsb("aTm", [128, 512], BF16)
                    sr = gsb("sr", [128, 512], F32)
                    ss = gsb("ss", [128, 4], F32)
                    yb = gsb("yb", [128, 512], BF16)
                    Lsave = gsb("Lsave", [128, 512], F32)
                    Ptot = gsb("Ptot", [128, 4], F32)
                    EX = gsb("EX", [128, 516], F32)
                    Gr = gsb("Gr", [128, 2, 516], F32)
                    Xs = gsb("Xs", [128, 512], F32)
                    Ep = gsb("Ep", [128, 512], F32)
                    Ap = gsb("Ap", [128, 4], F32)
                    P.add('dve', lambda: nc.vector.memset(S[:], 0.0), w=['S'])

                    def gla_tile(ul, ti, outputs, meta=False, ptot=False):
                        u_ = lambda c: ust[:, c, ul:ul + 128]
                        for c in range(8):
                            P.add('pe', (lambda c=c: nc.tensor.matmul(
                                banks[0][:, 0:256], u_(c), WGk[:, c, :], start=(c == 0), stop=(c == 7))),
                                r=['ust', 'WGk'], w=['bank0'])
                        for c in range(8):
                            P.add('pe', (lambda c=c: nc.tensor.matmul(
                                banks[1][:, 0:512], u_(c), WGv[:, c, :], start=(c == 0), stop=(c == 7))),
                                r=['ust', 'WGv'], w=['bank1'])
                        for c in range(8):
                            P.add('pe', (lambda c=c: nc.tensor.matmul(
                                banks[3][0:16, 256:384], WGl[:, c, :], u_(c), start=(c == 0), stop=(c == 7))),
                                r=['ust', 'WGl'], w=['bank3'])
                        P.add('act', lambda: nc.scalar.copy(out=glT[:], in_=banks[3][0:16, 256:384]),
                              r=['bank3'], w=['glT'])
                        P.add('pe', lambda: nc.tensor.matmul(
                            banks[0][:, 256:512], glT[:], gwb[:], start=True, stop=False),
                            r=['glT', 'gwb'], w=['bank0'])
                        P.add('pe', lambda: nc.tensor.matmul(
                            banks[0][:, 256:512], ones[0:1, 0:128], gbb[:], start=False, stop=True),
                            r=['ones', 'gbb'], w=['bank0'])
                        P.add('act', lambda: nc.scalar.activation(
                            out=ez[:], in_=banks[0][:, 256:512], func=AF.Exp, scale=-1.0),
                            r=['bank0'], w=['ez'])
                        P.add('act', lambda: nc.scalar.activation(
                            out=ez[:], in_=ez[:], func=AF.Ln, bias=C('one'), scale=1.0),
                            r=['ez', 'cst'], w=['ez'])
                        if meta:
                            P.add('dve', lambda: nc.vector.tensor_scalar(
                                out=la[:], in0=ez[:], scalar1=-1.0 / 16.0, scalar2=C('tokm'),
                                op0=ALU.mult, op1=ALU.mult), r=['ez', 'cst'], w=['la'])
                        else:
                            P.add('dve', lambda: nc.vector.tensor_scalar(
                                out=la[:], in0=ez[:], scalar1=-1.0 / 16.0, scalar2=None,
                                op0=ALU.mult), r=['ez'], w=['la'])
                        for d0 in (0, 64):
                            P.add('dve', (lambda d0=d0: nc.vector.tensor_copy(
                                out=lad[:, :, d0:d0 + 64], in_=la[:].rearrange("p (h d) -> p h d", h=4))),
                                r=['la'], w=['lad'])
                        for h in range(4):
                            P.add('pe', (lambda h=h: nc.tensor.matmul(
                                banks[2][:, h * 128:(h + 1) * 128], lad[:, h, :], C('U2'),
                                start=True, stop=True)), r=['lad', 'cst'], w=['bank2'])
                        P.add('pe', lambda: nc.tensor.matmul(
                            banks[3][:, 0:256], C('SU'), la[:], start=True, stop=True),
                            r=['la', 'cst'], w=['bank3'])
                        P.add('act', lambda: nc.scalar.activation(
                            out=ee1[:], in_=banks[3][:, 0:256], func=AF.Exp), r=['bank3'], w=['ee1'])
                        P.add('act', lambda: nc.scalar.activation(
                            out=dec[:], in_=banks[2][:, :].rearrange("p (h t) -> p h t", h=4)[:, :, 63::64],
                            func=AF.Exp), r=['bank2'], w=['dec'])
                        if outputs:
                            P.add('act', lambda: nc.scalar.activation(
                                out=eb[:], in_=banks[2][:, :], func=AF.Exp), r=['bank2'], w=['eb'])
                            P.add('act', lambda: nc.scalar.activation(
                                out=enb[:], in_=banks[2][:, :], func=AF.Exp, scale=-1.0),
                                r=['bank2'], w=['enb'])
                        for d0 in (0, 64):
                            P.add('dve', (lambda d0=d0: nc.vector.tensor_tensor(
                                out=kdd[:, :, d0:d0 + 64],
                                in0=banks[0][:, 0:256].rearrange("p (h d) -> p h d", h=4),
                                in1=ee1[:].rearrange("p (h d) -> p h d", h=4), op=ALU.mult)),
                                r=['bank0', 'ee1'], w=['kdd'])
                        P.add('act', lambda: nc.scalar.copy(out=vg[:], in_=banks[1][:, :]),
                              r=['bank1'], w=['vg'])
                        for (bk, lo) in ((4, 0), (5, 64)):
                            for h in range(4):
                                P.add('pe', (lambda bk=bk, lo=lo, h=h: nc.tensor.matmul(
                                    banks[bk][:, h * 128:(h + 1) * 128], kdd[lo:lo + 64, h, :],
                                    vg[lo:lo + 64, h * 128:(h + 1) * 128], start=True, stop=True)),
                                    r=['kdd', 'vg'], w=['bank%d' % bk])
                        if outputs:
                            P.add('dve', lambda: nc.vector.tensor_copy(out=Y[0:64, :], in_=S[0:64, :]),
                                  r=['S'], w=['Y'])
                        for h in range(4):
                            P.add('dve', (lambda h=h: nc.vector.scalar_tensor_tensor(
                                out=S[:, h * 128:(h + 1) * 128], in0=S[:, h * 128:(h + 1) * 128],
                                scalar=dec[:, h, 0:1], in1=banks[4][:, h * 128:(h + 1) * 128],
                                op0=ALU.mult, op1=ALU.add)), r=['S', 'dec', 'bank4'], w=['S'])
                        if outputs:
                            P.add('dve', lambda: nc.vector.tensor_copy(out=Y[64:128, :], in_=S[64:128, :]),
                                  r=['S'], w=['Y'])
                        for h in range(4):
                            P.add('dve', (lambda h=h: nc.vector.scalar_tensor_tensor(
                                out=S[:, h * 128:(h + 1) * 128], in0=S[:, h * 128:(h + 1) * 128],
                                scalar=dec[:, h, 1:2], in1=banks[5][:, h * 128:(h + 1) * 128],
                                op0=ALU.mult, op1=ALU.add)), r=['S', 'dec', 'bank5'], w=['S'])
                        if ptot:
                            for j in (0, 1):
                                P.add('dve', (lambda j=j: nc.vector.tensor_tensor(
                                    out=Ptot[:], in0=Ptot[:], in1=dec[:, :, j], op=ALU.mult)),
                                    r=['Ptot', 'dec'], w=['Ptot'])
                        if not outputs:
                            return
                        for (bk, Wd, key) in ((4, WGqd, 'WGqd'), (5, WGkd, 'WGkd')):
                            for h in range(4):
                                for c in range(8):
                                    P.add('pe', (lambda bk=bk, Wd=Wd, h=h, c=c: nc.tensor.matmul(
                                        banks[bk][:, h * 128:(h + 1) * 128], Wd[:, c, h, :], u_(c),
                                        start=(c == 0), stop=(c == 7))), r=['ust', key], w=['bank%d' % bk])
                        P.add('dve', lambda: nc.vector.scalar_tensor_tensor(
                            out=tq[:], in0=banks[4][:, :], scalar=0.125, in1=eb[:],
                            op0=ALU.mult, op1=ALU.mult), r=['bank4', 'eb'], w=['tq'])
                        P.add('dve', lambda: nc.vector.tensor_tensor(
                            out=X[:], in0=tq[:], in1=C('bmask'), op=ALU.mult), r=['tq', 'cst'], w=['X'])
                        P.add('dve', lambda: nc.vector.tensor_tensor(
                            out=keT[:], in0=banks[5][:, :], in1=enb[:], op=ALU.mult),
                            r=['bank5', 'enb'], w=['keT'])
                        for h in range(4):
                            P.add('pe', (lambda h=h: nc.tensor.matmul(
                                banks[0][:, h * 128:(h + 1) * 128], keT[:, h * 128:(h + 1) * 128],
                                X[:, h * 128:(h + 1) * 128], start=True, stop=True)),
                                r=['keT', 'X'], w=['bank0'])
                        P.add('dve', lambda: nc.vector.tensor_tensor(
                            out=aTm[:], in0=banks[0][:, :], in1=C('tmask'), op=ALU.mult),
                            r=['bank0', 'cst'], w=['aTm'])
                        for c in range(8):
                            P.add('pe', (lambda c=c: nc.tensor.matmul(
                                banks[1][:, 0:512], u_(c), WGr[:, c, :], start=(c == 0), stop=(c == 7))),
                                r=['ust', 'WGr'], w=['bank1'])
                        P.add('act', lambda: nc.scalar.activation(out=sr[:], in_=banks[1][:, :], func=AF.Silu),
                              r=['bank1'], w=['sr'])
                        for h in range(4):
                            hs = slice(h * 128, (h + 1) * 128)
                            P.add('pe', (lambda hs=hs: nc.tensor.matmul(
                                banks[1][:, hs], aTm[:, hs], vg[:, hs], start=True, stop=False)),
                                r=['aTm', 'vg'], w=['bank1'])
                            P.add('pe', (lambda hs=hs: nc.tensor.matmul(
                                banks[1][:, hs], X[:, hs], Y[:, hs], start=False, stop=True)),
                                r=['X', 'Y'], w=['bank1'])
                        P.add('act', lambda: nc.scalar.activation(out=tq[:], in_=banks[1][:, :], func=AF.Square),
                              r=['bank1'], w=['tq'])
                        P.add('dve', lambda: nc.vector.tensor_reduce(
                            out=ss[:], in_=tq[:].rearrange("p (h v) -> p h v", h=4), axis=AX.X, op=ALU.add),
                            r=['tq'], w=['ss'])
                        P.add('act', lambda: nc.scalar.activation(
                            out=ss[:], in_=ss[:], func=AF.Sqrt, bias=epsb[:, 0:1], scale=1.0 / 128.0),
                            r=['ss', 'epsb'], w=['ss'])
                        P.add('dve', lambda: nc.vector.reciprocal(out=ss[:], in_=ss[:]), r=['ss'], w=['ss'])
                        for h in range(4):
                            hs = slice(h * 128, (h + 1) * 128)
                            P.add('dve', (lambda hs=hs, h=h: nc.vector.scalar_tensor_tensor(
                                out=eb[:, hs], in0=banks[1][:, hs], scalar=ss[:, h:h + 1], in1=C('gn'),
                                op0=ALU.mult, op1=ALU.mult)), r=['bank1', 'ss', 'cst'], w=['eb'])
                        P.add('dve', lambda: nc.vector.tensor_tensor(
                            out=yb[:], in0=eb[:], in1=sr[:], op=ALU.mult), r=['eb', 'sr'], w=['yb'])
                        for k4 in range(4):
                            P.add('pe', (lambda k4=k4: nc.tensor.transpose(
                                ptb[:, k4 * 128:(k4 + 1) * 128], yb[:, k4 * 128:(k4 + 1) * 128], ident[:])),
                                r=['yb', 'ident'], w=['bankT'])
                        P.add('act', (lambda ti=ti: nc.scalar.copy(
                            out=ybT[:, :, ti * 128:(ti + 1) * 128],
                            in_=ptb[:, 0:512].rearrange("p (a b) -> p a b", a=4))),
                            r=['bankT'], w=['ybT%d' % ti])

                    norm_pass([(0, 128)], 1, sq, rs, "ng0", ust, 0, ust_keys)
                    gla_tile(0, -1, False, meta=True)
                    P.add('dve', lambda: nc.vector.tensor_scalar(
                        out=S[:], in0=S[:], scalar1=C('fmeta'), scalar2=None, op0=ALU.mult),
                        r=['S', 'cst'], w=['S'])
                    if not os.environ.get('KNOX'):
                        P.add('dve', lambda: nc.vector.tensor_copy(out=Lsave[:], in_=S[:]), r=['S'], w=['Lsave'])
                        P.add('dve', lambda: nc.vector.memset(Ptot[:], 1.0), w=['Ptot'])
                        for sti, (t0, nn) in enumerate(supertiles(OWN0, T_ALL)):
                            norm_pass([(t0, nn)], 1, sq, rs, "ng1", ust, t0, ust_keys)
                            for j in range(4):
                                gla_tile(j * 128, sti * 4 + j, False, ptot=True)
                        P.add('dve', lambda: nc.vector.tensor_copy(out=EX[:, 0:512], in_=S[:]), r=['S'], w=['EX'])
                        P.add('dve', lambda: nc.vector.tensor_copy(out=EX[:, 512:516], in_=Ptot[:]),
                              r=['Ptot'], w=['EX'])
                        P.add('sp', lambda: nc.sync.dma_start(out=cc_in.ap(), in_=EX[:]), r=['EX'], w=['cc_in'], dma=True)
                        P.add('pool', lambda: nc.gpsimd.collective_compute(
                            "AllGather", ALU.bypass, replica_groups=[list(range(N_CORES))],
                            ins=[cc_in.ap().opt()], outs=[cc_out.ap().opt()]),
                            r=['cc_in'], w=['cc_out'], dma='cc')
                        P.add('dve', lambda: nc.vector.memset(Xs[:], 0.0), w=['Xs'])
                        for r_ in range(N_CORES):
                            q = r_ % 2
                            P.add('sp', (lambda r_=r_, q=q: nc.sync.dma_start(
                                out=Gr[:, q, :], in_=cc_out.ap()[r_ * 128:(r_ + 1) * 128, :])),
                                r=['cc_out'], w=['Gr%d' % q], dma=True)
                            P.add('dve', (lambda r_=r_, q=q: nc.vector.tensor_scalar(
                                out=Ap[:], in0=Gr[:, q, 512:516], scalar1=C('sel')[:, r_:r_ + 1],
                                scalar2=C('nsel')[:, r_:r_ + 1], op0=ALU.mult, op1=ALU.add)),
                                r=['Gr%d' % q, 'cst'], w=['Ap'])
                            P.add('dve', (lambda r_=r_, q=q: nc.vector.tensor_scalar(
                                out=Ep[:], in0=Gr[:, q, 0:512], scalar1=C('sel')[:, r_:r_ + 1],
                                scalar2=None, op0=ALU.mult)), r=['Gr%d' % q, 'cst'], w=['Ep'])
                            for h in range(4):
                                hs = slice(h * 128, (h + 1) * 128)
                                P.add('dve', (lambda hs=hs, h=h: nc.vector.scalar_tensor_tensor(
                                    out=Xs[:, hs], in0=Xs[:, hs], scalar=Ap[:, h:h + 1], in1=Ep[:, hs],
                                    op0=ALU.mult, op1=ALU.add)), r=['Xs', 'Ap', 'Ep'], w=['Xs'])
                        P.add('dve', lambda: nc.vector.tensor_tensor(
                            out=S[:], in0=Lsave[:], in1=Xs[:], op=ALU.add), r=['Lsave', 'Xs'], w=['S'])
                    for sti, (t0, nn) in enumerate(supertiles(OWN0, T_ALL)):
                        norm_pass([(t0, nn)], 1, sq, rs, "ng", ust, t0, ust_keys)
                        for j in range(4):
                            gla_tile(j * 128, sti * 4 + j, True)
                    barrier("gla")
                if STAGE in (3, 4):
                    for k4 in range(4):
                        for (t0, nn) in supertiles(0, T_OWN):
                            P.add('dve', (lambda k4=k4, t0=t0, nn=nn: nc.vector.tensor_copy(
                                out=hT[:, k4, OWN0 + t0:OWN0 + t0 + nn], in_=yaT[:, k4, t0:t0 + nn])),
                                r=['yaT%d' % ti for ti in range(t0 // 128, (t0 + nn) // 128)],
                                w=hkeys(k4, OWN0 + t0, nn))
                    if STAGE == 4:
                        for k4 in range(4):
                            for (t0, nn) in supertiles(0, T_OWN):
                                P.add('dve', (lambda k4=k4, t0=t0, nn=nn: nc.vector.tensor_copy(
                                    out=hT[:, 4 + k4, OWN0 + t0:OWN0 + t0 + nn], in_=ybT[:, k4, t0:t0 + nn])),
                                    r=['ybT%d' % ti for ti in range(t0 // 128, (t0 + nn) // 128)],
                                    w=hkeys(4 + k4, OWN0 + t0, nn))

                pu.close()
                if STAGE >= 5:
                  with ExitStack() as pb:
                    def bsb(name, shape, dt):
                        return pb.enter_context(nc.sbuf_tensor("b_" + name, list(shape), dt))
                    uown = bsb("uown", [128, 8, T_OWN], BF16)
                    mixT = bsb("mixT", [128, 8, T_OWN], BF16)
                    wga = bsb("wga", [128, 1, 8, 128], BF16)
                    wgb = bsb("wgb", [128, 1, 8, 128], BF16)
                    wab = bsb("wab", [128, 1, 8, 128], BF16)
                    wo = bsb("wo", [128, 1, 8, 128], BF16)
                    sga = bsb("sga", [128, 2, 512], F32)
                    t1 = bsb("t1", [128, 512], F32)
                    t2 = bsb("t2", [128, 512], F32)
                    own_sts = supertiles(OWN0, T_ALL)

                    def ukeys_own(c, t0, nn):
                        return ['uo%d.%d' % (c, tt) for tt in range(t0 // 128, (t0 + nn) // 128)]
                    norm_pass(own_sts, 1, sq, rs, "nb", uown, OWN0, ukeys_own)

                    def mkeys(m, t0, nn):
                        return ['mx%d.%d' % (m, tt) for tt in range(t0 // 128, (t0 + nn) // 128)]
                    for m in range(8):
                        sl = 0
                        load_cols(lambda c0, w_, sl=sl: [(wga[:, sl, :, :], 0, 128)], 2320 + m * 128, 128, 'wga%d' % sl)
                        load_cols(lambda c0, w_, sl=sl: [(wgb[:, sl, :, :], 0, 128)], 3344 + m * 128, 128, 'wgb%d' % sl)
                        load_cols(lambda c0, w_, sl=sl: [(wab[:, sl, :, :], 0, 128)], m * 128, 128, 'wab%d' % sl, src=wab_d)
                        for (t0, nn) in own_sts:
                            o0 = t0 - OWN0
                            for br, (wg_, gk) in enumerate(((wga, 'wga%d' % sl), (wgb, 'wgb%d' % sl))):
                                yT, ykey = (yaT, 'yaT') if br == 0 else (ybT, 'ybT')
                                bA = nb()
                                for k4 in range(4):
                                    P.add('pe', (lambda bA=bA, k4=k4, br=br, yT=yT, o0=o0, nn=nn, sl=sl: nc.tensor.matmul(
                                        banks[bA][:, :nn], wab[:, sl, br * 4 + k4, :], yT[:, k4, o0:o0 + nn],
                                        start=(k4 == 0), stop=(k4 == 3))),
                                        r=['wab%d' % sl] + ['%s%d' % (ykey, tt) for tt in range(o0 // 128, (o0 + nn) // 128)],
                                        w=[BK(bA)])
                                bG = nb()
                                for c in range(8):
                                    P.add('pe', (lambda bG=bG, c=c, wg_=wg_, t0=t0, nn=nn, sl=sl: nc.tensor.matmul(
                                        banks[bG][:, :nn], wg_[:, sl, c, :], uown[:, c, t0 - OWN0:t0 - OWN0 + nn],
                                        start=(c == 0), stop=(c == 7))),
                                        r=[gk] + ukeys_own(c, t0, nn), w=[BK(bG)])
                                P.add('act', (lambda bG=bG, br=br, nn=nn: nc.scalar.activation(
                                    out=sga[:, br, :nn], in_=banks[bG][:, :nn], func=AF.Sigmoid)),
                                    r=[BK(bG)], w=['sga%d' % br])
                                tt_ = t1 if br == 0 else t2
                                P.add('dve', (lambda bA=bA, br=br, tt_=tt_, nn=nn: nc.vector.tensor_tensor(
                                    out=tt_[:, :nn], in0=sga[:, br, :nn], in1=banks[bA][:, :nn], op=ALU.mult)),
                                    r=['sga%d' % br, BK(bA)], w=['t%d' % (br + 1)])
                            P.add('dve', (lambda m=m, o0=o0, nn=nn: nc.vector.tensor_tensor(
                                out=mixT[:, m, o0:o0 + nn], in0=t1[:, :nn], in1=t2[:, :nn], op=ALU.add)),
                                r=['t1', 't2'], w=mkeys(m, o0, nn))
                    for m in range(8):
                        sl = 0
                        load_cols(lambda c0, w_, sl=sl: [(wo[:, sl, :, :], 0, 128)], m * 128, 128, 'wo%d' % sl, src=wout_d)
                        for (t0, nn) in own_sts:
                            o0 = t0 - OWN0
                            bO = nb()
                            for c in range(8):
                                P.add('pe', (lambda bO=bO, c=c, o0=o0, nn=nn, sl=sl: nc.tensor.matmul(
                                    banks[bO][:, :nn], wo[:, sl, c, :], mixT[:, c, o0:o0 + nn],
                                    start=(c == 0), stop=(c == 7))),
                                    r=['wo%d' % sl] + mkeys(c, o0, nn), w=[BK(bO)])
                            P.add('dve', (lambda bO=bO, m=m, t0=t0, nn=nn: nc.vector.tensor_tensor(
                                out=hT[:, m, t0:t0 + nn], in0=banks[bO][:, :nn], in1=hT[:, m, t0:t0 + nn],
                                op=ALU.add)), r=[BK(bO)] + hkeys(m, t0, nn), w=hkeys(m, t0, nn))
                barrier("mix")

            if STAGE >= 2 and not os.environ.get('KNOF2'):
                with ExitStack() as p2:
                    uT = p2.enter_context(nc.sbuf_tensor("uT2", [128, 8, T_ALL], BF16))
                    norm_pass(supertiles(OWN0, T_ALL), 2, sq, rs, "n2", uT)
                    ffn(supertiles(OWN0, T_ALL), ffn_w_d[1], "f2", uT)

            outb = ns.enter_context(nc.sbuf_tensor("outb", [128, 2, 8, 512], F32))
            oc = 0
            for (t0, nn) in supertiles(OWN0, T_ALL):
                bS = 6
                o = oc % 2
                oc += 1
                for c in range(8):
                    q = c % 2
                    P.add('act', (lambda c=c, q=q, t0=t0, nn=nn: nc.scalar.activation(
                        out=sq[:, q, :nn], in_=hT[:, c, t0:t0 + nn], func=AF.Square)),
                        r=hkeys(c, t0, nn), w=['sq%d' % q])
                    P.add('pe', (lambda c=c, q=q, nn=nn: nc.tensor.matmul(
                        banks[bS][:, :nn], ones[:], sq[:, q, :nn], start=(c == 0), stop=(c == 7))),
                        r=['ones', 'sq%d' % q], w=['bank%d' % bS])
                P.add('act', (lambda nn=nn: nc.scalar.activation(
                    out=rs[:, :nn], in_=banks[bS][:, :nn], func=AF.Sqrt, bias=epsb[:, 0:1],
                    scale=1.0 / D)), r=['bank%d' % bS, 'epsb'], w=['rs'])
                P.add('dve', (lambda nn=nn: nc.vector.reciprocal(
                    out=rs[:, :nn], in_=rs[:, :nn])), r=['rs'], w=['rs'])
                for c in range(8):
                    P.add('dve', (lambda c=c, o=o, t0=t0, nn=nn: nc.vector.scalar_tensor_tensor(
                        out=outb[:, o, c, :nn], in0=hT[:, c, t0:t0 + nn],
                        scalar=norms[:, 3, c:c + 1], in1=rs[:, :nn],
                        op0=ALU.mult, op1=ALU.mult)),
                        r=hkeys(c, t0, nn) + ['rs', 'norms'], w=['outb%d' % o])
                P.add('sp', (lambda o=o, t0=t0, nn=nn: nc.sync.dma_start(
                    out=out_d[:, :, t0 - OWN0:t0 - OWN0 + nn], in_=outb[:, o, :, :nn])),
                    r=['outb%d' % o], w=['OUT%d' % t0], dma=True)
            P.add('sp', lambda: nc.sync.nop(), r=['OUT%d' % t0 for (t0, nn) in supertiles(OWN0, T_ALL)])
            P.emit()
    return nc, P


def _fm(a):
    T = a.shape[0]
    return np.ascontiguousarray(a.reshape(T, 8, 128).transpose(2, 1, 0))


def _ffn_layout(wg, wu, wd):
    def gl(w):
        a = w.reshape(8, 128, NG, G, 128).transpose(2, 1, 3, 0, 4)
        return np.ascontiguousarray(a).reshape(NG, 128, G * 1024)
    d = wd.reshape(NG, G, 128, 1024).transpose(0, 2, 1, 3)
    return gl(wg), gl(wu), np.ascontiguousarray(d).reshape(NG, 128, G * 1024)


_CACHE = {}


def kernel(x, meta_tokens, ffn1_norm, ffn1_w_gate, ffn1_w_up, ffn1_w_down, mix_norm, w_in,
           gla_gate_w, gla_gate_b, gla_out_norm, att_sinks, w_branch_att, w_branch_gla, w_out,
           ffn2_norm, ffn2_w_gate, ffn2_w_up, ffn2_w_down, final_norm):
    f32 = np.float32
    x = np.asarray(x, f32)
    if 'nc' not in _CACHE:
        _CACHE['nc'] = build_nc()
    nc, P = _CACHE['nc']

    shared = {}
    nrm = np.stack([np.asarray(ffn1_norm, f32)[0], np.asarray(mix_norm, f32)[0],
                    np.asarray(ffn2_norm, f32)[0], np.asarray(final_norm, f32)], 0)
    shared["norms"] = np.ascontiguousarray(nrm.reshape(4, 8, 128).transpose(2, 0, 1))
    for l, (wg, wu, wd) in enumerate(((ffn1_w_gate, ffn1_w_up, ffn1_w_down),
                                      (ffn2_w_gate, ffn2_w_up, ffn2_w_down)), 1):
        a, b, c = _ffn_layout(np.asarray(wg, f32)[0], np.asarray(wu, f32)[0], np.asarray(wd, f32)[0])
        shared["f%d_wg" % l], shared["f%d_wu" % l], shared["f%d_wd" % l] = a, b, c

    W_in = np.asarray(w_in, f32)[0]
    shared["win"] = np.ascontiguousarray(W_in.reshape(8, 128, 4368).transpose(1, 0, 2))
    wa = np.asarray(w_branch_att, f32)[0].reshape(4, 128, 1024).transpose(1, 0, 2)
    wb = np.asarray(w_branch_gla, f32)[0].reshape(4, 128, 1024).transpose(1, 0, 2)
    shared["wab"] = np.ascontiguousarray(np.concatenate([wa, wb], 1))
    shared["wo"] = np.ascontiguousarray(np.asarray(w_out, f32)[0].reshape(8, 128, 1024).transpose(1, 0, 2))
    shared["gw"] = np.ascontiguousarray(np.concatenate(
        [np.asarray(gla_gate_w, f32)[0], np.asarray(gla_gate_b, f32)[0][None]], 0))
    BIG = 1.0e7
    p = np.arange(128)
    dist = np.zeros((128, 272), f32)
    kpos = np.arange(256) - 128
    dd = np.abs(p[:, None] - kpos[None, :]).astype(f32)
    qc = p[:, None] // 64
    kc = np.floor_divide(kpos[None, :], 64)
    valid = (kc <= qc) & (kc >= qc - 2)
    dist[:, :256] = np.where(valid, dd, BIG)
    dist1 = dist.copy()
    dist1[:, :128] = BIG
    same = (p[:, None] // 64) == (p[None, :] // 64)
    U2 = (same & (p[:, None] <= p[None, :])).astype(f32)
    SU = (same & (p[:, None] > p[None, :])).astype(f32)
    bm = ((p[:, None] // 64) == (p[None, :] // 64)).astype(f32)
    cst = np.zeros((128, CW), f32)

    def put(name, a):
        o, w_ = COFF[name]
        cst[:, o:o + w_] = a
    put('dist', dist)
    put('ident', np.eye(128, dtype=f32))
    put('U2', U2)
    put('SU', SU)
    put('bmask', np.tile(bm, (1, 4)))
    put('tmask', np.tile(U2, (1, 4)))
    put('sinkb', np.broadcast_to(np.asarray(att_sinks, f32)[0][None, :], (128, 8)))
    put('gn', np.broadcast_to(np.asarray(gla_out_norm, f32)[0][None, :], (128, 128)))
    put('tokm', (p < 16).astype(f32)[:, None])
    put('one', np.ones((128, 1), f32))

    meta = np.asarray(meta_tokens, f32)
    in_maps = []
    for core in range(N_CORES):
        b, s = divmod(core, 4)
        tok = np.zeros((T_ALL, D), f32)
        tok[0:16] = meta
        if s > 0:
            tok[128:256] = x[b, s * T_OWN - 128: s * T_OWN]
        tok[256:] = x[b, s * T_OWN:(s + 1) * T_OWN]
        m = dict(shared)
        m["xT"] = _fm(tok)
        cc = cst.copy()
        o, w_ = COFF['dist1']
        cc[:, o:o + w_] = dist1 if s == 0 else dist
        cc[:, COFF['fmeta'][0]] = 1.0 if s == 0 else 0.0
        for r in range(N_CORES):
            sv_ = 1.0 if (r // 4 == b and r < core) else 0.0
            cc[:, COFF['sel'][0] + r] = sv_
            cc[:, COFF['nsel'][0] + r] = 1.0 - sv_
        m["cst"] = cc
        in_maps.append(m)

    res = run_bass_kernel_spmd(nc, in_maps, core_ids=list(range(N_CORES)))
    out = np.empty((2, 8192, D), f32)
    for core in range(N_CORES):
        b, s = divmod(core, 4)
        oT = np.asarray(res.results[core]["outT"], f32)
        out[b, s * T_OWN:(s + 1) * T_OWN] = oT.transpose(2, 1, 0).reshape(T_OWN, D)
    return out
```

```python
import numpy as np
import concourse.bass as bass
import concourse.mybir as mybir
from concourse.bass_utils import run_bass_kernel_spmd

F32 = mybir.dt.float32
BF16 = mybir.dt.bfloat16
AF = mybir.ActivationFunctionType
ALU = mybir.AluOpType
AX = mybir.AxisListType

D = 1024
DFF = 2816
NF = DFF // 128
G = 2
NG = NF // G
T_ALL = 2304
T_OWN = 2048
OWN0 = 256
EPS = 1e-6
N_CORES = 8
ENGS = ['pe', 'act', 'dve', 'sp']

import os
STAGE = int(os.environ.get('KSTAGE', '99'))


class Prog:
    NS = 24

    def __init__(self, nc):
        self.nc = nc
        self.eng = {'pe': nc.tensor, 'act': nc.scalar, 'dve': nc.vector,
                    'pool': nc.gpsimd, 'sp': nc.sync}
        self.ops = []

    def add(self, eng, fn, r=(), w=(), dma=False):
        self.ops.append((eng, fn, tuple(r), tuple(w), dma))

    def emit(self):
        nc = self.nc
        ops = self.ops
        n = len(ops)
        last_w = {}
        readers = {}
        AENG = ENGS + ['pool']
        eng_n = {e: 0 for e in AENG}
        op_local = [0] * n
        clock = {e: {} for e in AENG}
        op_clock = [None] * n
        waits = [None] * n
        needed = set()
        dma_sem_of = {}
        dma_cnt = [0] * (self.NS + 1)
        ndma = 0
        for i, (e, fn, r, w, dma) in enumerate(ops):
            raw = set()
            deps = set()
            for k in r:
                j = last_w.get(k)
                if j is not None:
                    deps.add(j)
                    raw.add(j)
            for k in w:
                j = last_w.get(k)
                if j is not None:
                    deps.add(j)
                for j in readers.get(k, ()):
                    deps.add(j)
            deps.discard(i)
            ck = clock[e]
            li = eng_n[e]
            my_w = []
            for j in sorted(deps, reverse=True):
                ej, _, _, _, dmaj = ops[j]
                if dmaj:
                    s, v = dma_sem_of[j]
                    if ck.get(('d', s), 0) >= v:
                        continue
                    my_w.append(('d', s, v))
                else:
                    lj = op_local[j]
                    if ej == e:
                        if e == 'pe' or dma:
                            continue
                        if j not in raw or li - lj > 3:
                            continue
                        if ck.get(('self', e), -1) >= lj:
                            continue
                        my_w.append(('e', ej, j))
                        needed.add(j)
                        ck[('self', e)] = lj
                        continue
                    if ck.get(ej, -1) >= lj:
                        continue
                    my_w.append(('e', ej, j))
                    needed.add(j)
                for kk, vv in op_clock[j].items():
                    if ck.get(kk, -1) < vv:
                        ck[kk] = vv
            waits[i] = my_w
            oc = {kk: vv for kk, vv in ck.items() if not (isinstance(kk, tuple) and kk[0] == 'self')}
            if dma:
                if dma == 'cc':
                    s = self.NS
                    dma_cnt[s] += 1
                else:
                    s = ndma % self.NS
                    ndma += 1
                    dma_cnt[s] += 16
                dma_sem_of[i] = (s, dma_cnt[s])
                oc[('d', s)] = dma_cnt[s]
            else:
                op_local[i] = li
                eng_n[e] = li + 1
                oc[e] = li
            op_clock[i] = oc
            for k in r:
                readers.setdefault(k, []).append(i)
            for k in w:
                last_w[k] = i
                readers[k] = []
        del op_clock
        sems = {e: nc.alloc_semaphore(name='s_' + e) for e in AENG}
        dsems = [nc.alloc_semaphore(name='d%d' % s) for s in range(self.NS + 1)]
        sig = {e: 0 for e in AENG}
        sigval = {}
        for i, (e, fn, r, w, dma) in enumerate(ops):
            if i in needed:
                sig[e] += 1
                sigval[i] = sig[e]
        if os.environ.get('KSIM'):
            q = {e: [i for i in range(n) if ops[i][0] == e] for e in AENG}
            pos = {e: 0 for e in AENG}
            sv = {e: 0 for e in AENG}
            dv = [0] * (self.NS + 1)
            prog = True
            while prog:
                prog = False
                for e in AENG:
                    while pos[e] < len(q[e]):
                        i = q[e][pos[e]]
                        ok = True
                        for wt in waits[i]:
                            if wt[0] == 'd':
                                ok = ok and dv[wt[1]] >= wt[2]
                            else:
                                ok = ok and sv[wt[1]] >= sigval[wt[2]]
                        if not ok:
                            break
                        if ops[i][4]:
                            dv[dma_sem_of[i][0]] += (1 if ops[i][4] == 'cc' else 16)
                        elif i in needed:
                            sv[e] += 1
                        pos[e] += 1
                        prog = True
            for e in AENG:
                if pos[e] < len(q[e]):
                    i = q[e][pos[e]]
                    print("DEADLOCK", e, pos[e], len(q[e]), i, ops[i][2], ops[i][3], waits[i],
                          [(wt, sigval.get(wt[2])) for wt in waits[i] if wt[0] == 'e'], sv, dv)
            print("SIM done", pos)
        for i, (e, fn, r, w, dma) in enumerate(ops):
            E = self.eng[e]
            for wt in waits[i]:
                if wt[0] == 'd':
                    E.wait_ge(dsems[wt[1]], wt[2])
                else:
                    E.wait_ge(sems[wt[1]], sigval[wt[2]])
            step = 1 if dma == 'cc' else 16
            if dma and dma_sem_of[i][1] > step:
                E.wait_ge(dsems[dma_sem_of[i][0]], dma_sem_of[i][1] - step)
            ins = fn()
            if dma:
                ins.then_inc(dsems[dma_sem_of[i][0]], step)
            elif i in needed:
                ins.then_inc(sems[e], 1)
        self.stats = (n, {e: eng_n[e] for e in AENG}, ndma, len(needed))
        self.sigstats = (dict(sig), max(dma_cnt))


COFF = {}
_o = 0
for _n, _w in (('dist', 272), ('dist1', 272), ('ident', 128), ('U2', 128), ('SU', 128),
               ('bmask', 512), ('tmask', 512), ('sinkb', 8), ('gn', 128), ('tokm', 1), ('one', 1), ('fmeta', 1), ('sel', 8), ('nsel', 8)):
    COFF[_n] = (_o, _w)
    _o += _w
CW = _o


def supertiles(t0, t1, size=512, split_last=False):
    out = []
    t = t0
    while t < t1:
        nn = min(size, t1 - t)
        out.append((t, nn))
        t += nn
    if split_last and out[-1][1] == size:
        a, nn = out.pop()
        out.append((a, nn // 2))
        out.append((a + nn // 2, nn // 2))
    return out


def build_nc():
    nc = bass.Bass("TRN2", target_bir_lowering=False)
    P = Prog(nc)

    def din(name, shape):
        return nc.dram_tensor(name, list(shape), F32, kind="ExternalInput").ap()

    xT_d = din("xT", [128, 8, T_ALL])
    norms_d = din("norms", [128, 4, 8])
    ffn_w_d = []
    for l in (1, 2):
        ffn_w_d.append((din("f%d_wg" % l, [NG, 128, G * 1024]),
                        din("f%d_wu" % l, [NG, 128, G * 1024]),
                        din("f%d_wd" % l, [NG, 128, G * 1024])))
    out_d = nc.dram_tensor("outT", [128, 8, T_OWN], F32, kind="ExternalOutput").ap()

    from contextlib import ExitStack
    with ExitStack() as es:
        def sb(name, shape, dt):
            return es.enter_context(nc.sbuf_tensor("sb_" + name, list(shape), dt))

        def ps(name, shape, dt=F32):
            return es.enter_context(nc.psum_tensor(name, list(shape), dt))

        hT = sb("hT", [128, 8, T_ALL], F32)
        norms = sb("norms", [128, 4, 8], F32)
        ones = sb("ones", [128, 128], BF16)
        banks = [ps("bank%d" % i, [128, 512]) for i in range(7)]
        ptb = ps("bankT", [128, 1024], BF16)

        epsb = sb("epsb", [128, 1], F32)
        P.add('dve', lambda: nc.vector.memset(ones[:], 1.0), w=['ones'])
        P.add('dve', lambda: nc.vector.memset(epsb[:], EPS), w=['epsb'])
        P.add('sp', lambda: nc.sync.dma_start(out=norms[:], in_=norms_d), w=['norms'], dma=True)
        for c in range(8):
            for (t0, nn) in supertiles(0, T_ALL, 1152):
                P.add('sp', (lambda c=c, t0=t0, nn=nn: nc.sync.dma_start(
                    out=hT[:, c, t0:t0 + nn], in_=xT_d[:, c, t0:t0 + nn])),
                    w=['h%d.%d' % (c, tt) for tt in range(t0 // 128, (t0 + nn) // 128)], dma=True)

        def hkeys(c, t0, nn):
            return ['h%d.%d' % (c, tt) for tt in range(t0 // 128, (t0 + nn) // 128)]

        def ukeys_default(c, t0, nn):
            return ['u%d.%d' % (c, tt) for tt in range(t0 // 128, (t0 + nn) // 128)]

        def norm_pass(sts, gi, sq, rs, tag, uT, uoff=0, ukeys=None):
            ukeys = ukeys or ukeys_default
            cnt = [0]
            for (t0, nn) in sts:
                bS = 6
                for c in range(8):
                    q = cnt[0] % 2
                    cnt[0] += 1
                    P.add('act', (lambda c=c, q=q, t0=t0, nn=nn: nc.scalar.activation(
                        out=sq[:, q, :nn], in_=hT[:, c, t0:t0 + nn], func=AF.Square)),
                        r=hkeys(c, t0, nn), w=['sq%d' % q])
                    P.add('pe', (lambda c=c, q=q, nn=nn: nc.tensor.matmul(
                        banks[bS][:, :nn], ones[:], sq[:, q, :nn], start=(c == 0), stop=(c == 7))),
                        r=['ones', 'sq%d' % q], w=['bank%d' % bS])
                P.add('act', (lambda nn=nn: nc.scalar.activation(
                    out=rs[:, :nn], in_=banks[bS][:, :nn], func=AF.Sqrt, bias=epsb[:, 0:1],
                    scale=1.0 / D)), r=['bank%d' % bS, 'epsb'], w=['rs'])
                P.add('dve', (lambda nn=nn: nc.vector.reciprocal(
                    out=rs[:, :nn], in_=rs[:, :nn])), r=['rs'], w=['rs'])
                for c in range(8):
                    P.add('dve', (lambda c=c, t0=t0, nn=nn: nc.vector.scalar_tensor_tensor(
                        out=uT[:, c, t0 - uoff:t0 - uoff + nn], in0=hT[:, c, t0:t0 + nn],
                        scalar=norms[:, gi, c:c + 1], in1=rs[:, :nn],
                        op0=ALU.mult, op1=ALU.mult)),
                        r=hkeys(c, t0, nn) + ['rs', 'norms'], w=ukeys(c, t0, nn))

        def ffn(sts, wd3, tag, uT):
            ukeys = ukeys_default
            wg_d, wu_d, wd_d = wd3
            with ExitStack() as fs:
                def fsb(name, shape, dt):
                    return fs.enter_context(nc.sbuf_tensor(tag + name, list(shape), dt))
                stg = [fsb("stg%d" % k, [128, G * 1024], F32) for k in range(3)]
                wbf = [[fsb("wbf%d_%d" % (s, k), [128, G * 1024], BF16) for k in range(3)]
                       for s in range(2)]
                sg = fsb("sg", [128, 2, 512], F32)
                act = fsb("act", [128, 2 * G, 512], BF16)
                K = lambda s: tag + s
                gu_cnt = [0]

                def load_group(gi):
                    s = gi % 2
                    for k, wdr in enumerate((wg_d, wu_d, wd_d)):
                        P.add('sp', (lambda k=k, wdr=wdr, gi=gi: nc.sync.dma_start(
                            out=stg[k][:], in_=wdr[gi])), w=[K('stg%d' % k)], dma=True)
                        P.add('act', (lambda k=k, s=s: nc.scalar.copy(
                            out=wbf[s][k][:], in_=stg[k][:])),
                            r=[K('stg%d' % k)], w=[K('wbf%d_%d' % (s, k))])

                def gate_up(gi, sti):
                    s = gi % 2
                    t0, nn = sts[sti]
                    par = gu_cnt[0] % 2
                    gu_cnt[0] += 1
                    for g in range(G):
                        bG = g
                        bU = 2 + g
                        for k, bk in ((0, bG), (1, bU)):
                            for c in range(8):
                                P.add('pe', (lambda k=k, bk=bk, c=c, g=g, s=s, t0=t0, nn=nn: nc.tensor.matmul(
                                    banks[bk][:, :nn],
                                    wbf[s][k][:, g * 1024 + c * 128: g * 1024 + (c + 1) * 128],
                                    uT[:, c, t0:t0 + nn], start=(c == 0), stop=(c == 7))),
                                    r=[K('wbf%d_%d' % (s, k))] + ukeys(c, t0, nn), w=['bank%d' % bk])
                        P.add('act', (lambda bG=bG, g=g, nn=nn: nc.scalar.activation(
                            out=sg[:, g, :nn], in_=banks[bG][:, :nn], func=AF.Silu)),
                            r=['bank%d' % bG], w=[K('sg%d' % g)])
                        a = par * G + g
                        P.add('dve', (lambda a=a, g=g, bU=bU, nn=nn: nc.vector.tensor_tensor(
                            out=act[:, a, :nn], in0=sg[:, g, :nn], in1=banks[bU][:, :nn], op=ALU.mult)),
                            r=[K('sg%d' % g), 'bank%d' % bU], w=[K('act%d' % a)])
                    return par

                def down(gi, sti, par):
                    s = gi % 2
                    t0, nn = sts[sti]
                    for m in range(8):
                        bY = 4 + (m % 3)
                        for g in range(G):
                            a = par * G + g
                            P.add('pe', (lambda bY=bY, g=g, a=a, m=m, s=s, nn=nn: nc.tensor.matmul(
                                banks[bY][:, :nn],
                                wbf[s][2][:, g * 1024 + m * 128: g * 1024 + (m + 1) * 128],
                                act[:, a, :nn], start=(g == 0), stop=(g == G - 1))),
                                r=[K('wbf%d_2' % s), K('act%d' % a)], w=['bank%d' % bY])
                        P.add('dve', (lambda bY=bY, m=m, t0=t0, nn=nn: nc.vector.scalar_tensor_tensor(
                            out=hT[:, m, t0:t0 + nn], in0=banks[bY][:, :nn], scalar=0.5,
                            in1=hT[:, m, t0:t0 + nn], op0=ALU.mult, op1=ALU.add)),
                            r=['bank%d' % bY] + hkeys(m, t0, nn), w=hkeys(m, t0, nn))

                load_group(0)
                pend = None
                for gi in range(NG):
                    for sti in range(len(sts)):
                        par = gate_up(gi, sti)
                        if pend is not None:
                            down(*pend)
                        pend = (gi, sti, par)
                        if sti == 0 and gi + 1 < NG:
                            load_group(gi + 1)
                down(*pend)
                barrier(tag)

        bscr = sb("bscr", [128, 8], F32)
        bscr2 = sb("bscr2", [128, 8], F32)

        def tiny(e, col):
            if e == 'pe':
                return lambda: nc.tensor.matmul(banks[6][0:1, 0:1], ones[0:1, 0:1], ones[0:1, 0:1],
                                                start=True, stop=True)
            if e == 'act':
                return lambda: nc.scalar.copy(out=bscr[0:1, col:col + 1], in_=epsb[0:1, 0:1])
            if e == 'dve':
                return lambda: nc.vector.tensor_copy(out=bscr[0:1, col:col + 1], in_=epsb[0:1, 0:1])
            return lambda: nc.sync.dma_start(out=bscr2[0:1, col:col + 1], in_=epsb[0:1, 0:1])

        def barrier(tag):
            allk = ['__bar_' + e for e in ENGS]
            xr = {'pe': ['ones', 'bank6'], 'act': ['epsb'], 'dve': ['epsb'], 'sp': ['epsb']}
            xw = {'pe': ['bank6'], 'act': ['bscrA0'], 'dve': ['bscrD0'], 'sp': ['bscrS0']}
            xw2 = {'pe': ['bank6'], 'act': ['bscrA1'], 'dve': ['bscrD1'], 'sp': ['bscrS1']}
            col = {'pe': 0, 'act': 0, 'dve': 2, 'sp': 4}
            for e in ENGS:
                P.add(e, tiny(e, col[e]), r=xr[e], w=['__bar_' + e] + xw[e], dma=(e == 'sp'))
            for e in ENGS:
                P.add(e, tiny(e, col[e] + 1), r=allk + xr[e], w=xw2[e], dma=(e == 'sp'))

        SLOPES = [2.0 ** (-8.0 * (h + 1) / 8) for h in range(8)]
        cst_d = din("cst", [128, CW])
        win_d = din("win", [128, 8, 4368])
        wab_d = din("wab", [128, 8, 1024])
        wout_d = din("wo", [128, 8, 1024])
        cc_in = nc.dram_tensor("cc_in", [128, 516], F32)
        cc_out = nc.dram_tensor("cc_out", [N_CORES * 128, 516], F32)
        gw_d = din("gw", [17, 256])

        with ExitStack() as ns:
            sq = ns.enter_context(nc.sbuf_tensor("sq", [128, 2, 512], BF16))
            rs = ns.enter_context(nc.sbuf_tensor("rs", [128, 512], F32))
            cst = ns.enter_context(nc.sbuf_tensor("cstb", [128, CW], F32))
            P.add('sp', lambda: nc.sync.dma_start(out=cst[:], in_=cst_d), w=['cst'], dma=True)

            def C(name):
                o, wdt = COFF[name]
                return cst[:, o:o + wdt]

            with ExitStack() as p1:
                uT = p1.enter_context(nc.sbuf_tensor("uT1", [128, 8, T_ALL], BF16))
                if STAGE >= 1 and not os.environ.get('KSKIP1'):
                    norm_pass(supertiles(0, T_ALL), 0, sq, rs, "n1", uT)
                    ffn(supertiles(0, T_ALL), ffn_w_d[0], "f1", uT)

            if STAGE >= 3:
              with ExitStack() as pm:
                def msb(name, shape, dt):
                    return pm.enter_context(nc.sbuf_tensor("m_" + name, list(shape), dt))
                yaT = msb("yaT", [128, 4, T_OWN], BF16)
                ybT = msb("ybT", [128, 4, T_OWN], BF16)
                ident = msb("ident", [128, 128], BF16)
                gwb = msb("gwb", [16, 256], BF16)
                gbb = msb("gbb", [1, 256], BF16)
                gw32 = msb("gw32", [16, 256], F32)
                gb32 = msb("gb32", [1, 256], F32)
                P.add('dve', lambda: nc.vector.tensor_copy(out=ident[:], in_=C('ident')), r=['cst'], w=['ident'])
                P.add('sp', lambda: nc.sync.dma_start(out=gw32[:], in_=gw_d[0:16, :]), w=['gw32'], dma=True)
                P.add('sp', lambda: nc.sync.dma_start(out=gb32[:], in_=gw_d[16:17, :]), w=['gb32'], dma=True)
                P.add('dve', lambda: nc.vector.tensor_copy(out=gwb[:], in_=gw32[:]), r=['gw32'], w=['gwb'])
                P.add('dve', lambda: nc.vector.tensor_copy(out=gbb[:], in_=gb32[:]), r=['gb32'], w=['gbb'])
                rr = [0]

                def nb():
                    rr[0] = (rr[0] + 1) % 6
                    return rr[0]

                def BK(b):
                    return 'bank%s' % b

                stgw = msb("stgw", [128, 8, 128], F32)

                def load_cols(dst_fn, col0, ncols, key, src=None):
                    src = win_d if src is None else src
                    for c0 in range(0, ncols, 128):
                        w_ = min(128, ncols - c0)
                        P.add('sp', (lambda c0=c0, w_=w_: nc.sync.dma_start(
                            out=stgw[:, :, :w_], in_=src[:, :, col0 + c0:col0 + c0 + w_])),
                            w=['stgw'], dma=True)
                        for (dst, lo, hi) in dst_fn(c0, w_):
                            P.add('dve', (lambda dst=dst, lo=lo, hi=hi: nc.vector.tensor_copy(
                                out=dst, in_=stgw[:, :, lo:hi])), r=['stgw'], w=[key])

                pu = ExitStack()
                ust = pu.enter_context(nc.sbuf_tensor("m_ust", [128, 8, 512], BF16))

                def ust_keys(c, t0, nn):
                    return ['ust']

                if True:
                  with ExitStack() as pa:
                    def asb(name, shape, dt):
                        return pa.enter_context(nc.sbuf_tensor("a_" + name, list(shape), dt))
                    WQ = asb("WQ", [128, 8, 512], BF16)
                    WKd = asb("WKd", [128, 8, 2, 128], BF16)
                    WV = asb("WV", [128, 8, 128], BF16)
                    load_cols(lambda c0, w_: [(WQ[:, :, c0:c0 + w_], 0, w_)], 0, 512, 'WQ')
                    load_cols(lambda c0, w_: [(WKd[:, :, g, d0:d0 + 64], g * 64, g * 64 + 64)
                                              for g in range(2) for d0 in (0, 64)], 512, 128, 'WKd')
                    load_cols(lambda c0, w_: [(WV[:, :, :], 0, 128)], 640, 128, 'WV')
                    kbuf = asb("kbuf", [128, 4, 2, 128], BF16)
                    vbuf = asb("vbuf", [128, 4, 128], BF16)
                    qT = asb("qT", [128, 2, 4, 128], BF16)
                    Sb = asb("Sb", [128, 2, 272], F32)
                    Pb = asb("Pb", [128, 2, 272], BF16)
                    PT = asb("PT", [128, 2, 3, 128], BF16)
                    mx = asb("mx", [128, 8], F32)
                    negm = asb("negm", [128, 8], F32)
                    rsum = asb("rsum", [128, 8], F32)
                    es = asb("es", [128, 8], F32)
                    ya = asb("ya", [128, 512], BF16)
                    ust2 = asb("ust2", [128, 8, 512], BF16)
                    ustb = [ust, ust2]
                    rr4 = [0]

                    def nb4():
                        rr4[0] = (rr4[0] + 1) % 4
                        return rr4[0]

                    def kv_tile(U, uk, ul, slot):
                        for g in range(2):
                            b = nb4()
                            for c in range(8):
                                P.add('pe', (lambda b=b, g=g, c=c: nc.tensor.matmul(
                                    banks[b][:, 0:128], WKd[:, c, g, :], U[:, c, ul:ul + 128],
                                    start=(c == 0), stop=(c == 7))), r=['WKd', uk], w=[BK(b)])
                            P.add('act', (lambda b=b, g=g: nc.scalar.copy(
                                out=kbuf[:, slot, g, :], in_=banks[b][:, 0:128])),
                                r=[BK(b)], w=['kbuf%d' % slot])
                        b = nb4()
                        for c in range(8):
                            P.add('pe', (lambda b=b, c=c: nc.tensor.matmul(
                                banks[b][:, 0:128], U[:, c, ul:ul + 128], WV[:, c, :],
                                start=(c == 0), stop=(c == 7))), r=['WV', uk], w=[BK(b)])
                        P.add('act', (lambda b=b: nc.scalar.copy(
                            out=vbuf[:, slot, :], in_=banks[b][:, 0:128])), r=[BK(b)], w=['vbuf%d' % slot])

                    def q_tile(U, uk, ul, qs):
                        for ci in range(4):
                            b = nb4()
                            for c in range(8):
                                P.add('pe', (lambda b=b, ci=ci, c=c: nc.tensor.matmul(
                                    banks[b][:, 0:128], WQ[:, c, ci * 128:(ci + 1) * 128],
                                    U[:, c, ul:ul + 128], start=(c == 0), stop=(c == 7))),
                                    r=['WQ', uk], w=[BK(b)])
                            P.add('act', (lambda b=b, ci=ci: nc.scalar.mul(
                                out=qT[:, qs, ci, :], in_=banks[b][:, 0:128], mul=0.125)),
                                r=[BK(b)], w=['qT%d' % qs])

                    def head_A(h, ti, cur, prev, qs):
                        dist = C('dist1') if ti == 0 else C('dist')
                        g, ci, r0, q = h // 4, h // 2, (h % 2) * 64, h % 2
                        b = nb4()
                        for (slot, c0, w_) in ((prev, 0, 128), (cur, 128, 128), (3, 256, 16)):
                            P.add('pe', (lambda b=b, slot=slot, c0=c0, w_=w_, g=g, ci=ci, r0=r0: nc.tensor.matmul(
                                banks[b][:, c0:c0 + w_], qT[r0:r0 + 64, qs, ci, :],
                                kbuf[r0:r0 + 64, slot, g, 0:w_], start=True, stop=True)),
                                r=['qT%d' % qs, 'kbuf%d' % slot], w=[BK(b)])
                        P.add('dve', (lambda b=b, q=q, h=h, dist=dist: nc.vector.scalar_tensor_tensor(
                            out=Sb[:, q, :], in0=dist, scalar=-SLOPES[h], in1=banks[b][:, 0:272],
                            op0=ALU.mult, op1=ALU.add)), r=[BK(b), 'cst'], w=['Sb%d' % q])
                        P.add('dve', (lambda q=q, h=h: nc.vector.tensor_reduce(
                            out=mx[:, h:h + 1], in_=Sb[:, q, :], axis=AX.X, op=ALU.max)),
                            r=['Sb%d' % q], w=['mx%d' % h])
                        P.add('dve', (lambda h=h: nc.vector.tensor_scalar(
                            out=negm[:, h:h + 1], in0=mx[:, h:h + 1], scalar1=C('sinkb')[:, h:h + 1],
                            scalar2=-1.0, op0=ALU.max, op1=ALU.mult)), r=['mx%d' % h, 'cst'], w=['negm%d' % h])
                        P.add('act', (lambda q=q, h=h: nc.scalar.activation(
                            out=Pb[:, q, :], in_=Sb[:, q, :], func=AF.Exp, bias=negm[:, h:h + 1],
                            scale=1.0, accum_out=rsum[:, h:h + 1])),
                            r=['Sb%d' % q, 'negm%d' % h], w=['Pb%d' % q, 'rsum%d' % h])

                    def head_B(h, ti, cur, prev, bo):
                        g, q = h // 4, h % 2
                        pq = q * 512
                        for (blk, c0, w_) in ((0, 0, 128), (1, 128, 128), (2, 256, 16)):
                            P.add('pe', (lambda blk=blk, c0=c0, w_=w_, q=q, pq=pq: nc.tensor.transpose(
                                ptb[0:w_, pq + blk * 128:pq + (blk + 1) * 128], Pb[:, q, c0:c0 + w_], ident[:])),
                                r=['Pb%d' % q, 'ident'], w=['bankT%d' % q])
                        P.add('dve', (lambda q=q, pq=pq: nc.vector.tensor_copy(
                            out=PT[:, q, 0:2, :], in_=ptb[:, pq:pq + 256].rearrange("p (a b) -> p a b", a=2))),
                            r=['bankT%d' % q], w=['PT%d' % q])
                        P.add('dve', (lambda q=q, pq=pq: nc.vector.tensor_copy(
                            out=PT[0:16, q, 2, :], in_=ptb[0:16, pq + 256:pq + 384])), r=['bankT%d' % q], w=['PT%d' % q])
                        for (blk, slot, kk) in ((0, prev, 128), (1, cur, 128), (2, 3, 16)):
                            P.add('pe', (lambda bo=bo, blk=blk, slot=slot, kk=kk, q=q, g=g, h=h: nc.tensor.matmul(
                                banks[bo][:, h * 64:(h + 1) * 64], PT[0:kk, q, blk, :],
                                vbuf[0:kk, slot, g * 64:(g + 1) * 64], start=(blk == 0), stop=(blk == 2))),
                                r=['PT%d' % q, 'vbuf%d' % slot], w=[BK(bo)])

                    def att_finish(ti, bo):
                        allnegm = ['negm%d' % h for h in range(8)]
                        P.add('dve', lambda: nc.vector.tensor_tensor(
                            out=es[:], in0=C('sinkb'), in1=negm[:], op=ALU.add), r=allnegm + ['cst'], w=['es'])
                        P.add('act', lambda: nc.scalar.activation(out=es[:], in_=es[:], func=AF.Exp),
                              r=['es'], w=['es'])
                        P.add('dve', lambda: nc.vector.tensor_tensor(
                            out=es[:], in0=es[:], in1=rsum[:], op=ALU.add),
                            r=['es'] + ['rsum%d' % h for h in range(8)], w=['es'])
                        P.add('dve', lambda: nc.vector.reciprocal(out=es[:], in_=es[:]), r=['es'], w=['es'])
                        for h in range(8):
                            P.add('dve', (lambda bo=bo, h=h: nc.vector.tensor_scalar(
                                out=ya[:, h * 64:(h + 1) * 64], in0=banks[bo][:, h * 64:(h + 1) * 64],
                                scalar1=es[:, h:h + 1], scalar2=None, op0=ALU.mult)),
                                r=[BK(bo), 'es'], w=['ya'])
                        for k4 in range(4):
                            P.add('pe', (lambda k4=k4: nc.tensor.transpose(
                                ptb[:, k4 * 128:(k4 + 1) * 128], ya[:, k4 * 128:(k4 + 1) * 128], ident[:])),
                                r=['ya', 'ident'], w=['bankT0'])
                        P.add('act', (lambda ti=ti: nc.scalar.copy(
                            out=yaT[:, :, ti * 128:(ti + 1) * 128],
                            in_=ptb[:, 0:512].rearrange("p (a b) -> p a b", a=4))),
                            r=['bankT0'], w=['yaT%d' % ti])

                    own_sts_a = supertiles(OWN0, T_ALL)

                    def front(ti):
                        sti, j = divmod(ti, 4)
                        U = ustb[sti % 2]
                        uk = 'ust%d' % (sti % 2)
                        if j == 0:
                            t0, nn = own_sts_a[sti]
                            norm_pass([(t0, nn)], 1, sq, rs, "na", U, t0, (lambda c, a, b, uk=uk: [uk]))
                        kv_tile(U, uk, j * 128, ti % 3)
                        q_tile(U, uk, j * 128, ti % 2)

                    norm_pass([(0, 256)], 1, sq, rs, "na0", ust2, 0, (lambda c, a, b: ['ust1']))
                    kv_tile(ust2, 'ust1', 0, 3)
                    kv_tile(ust2, 'ust1', 128, 2)
                    front(0)
                    for ti in range(16):
                        cur, prev, qs = ti % 3, (ti - 1) % 3, ti % 2
                        bo = 4 + (ti % 2)
                        head_A(0, ti, cur, prev, qs)
                        for h in range(1, 8):
                            head_A(h, ti, cur, prev, qs)
                            if h == 4 and ti + 1 < 16:
                                front(ti + 1)
                            head_B(h - 1, ti, cur, prev, bo)
                        head_B(7, ti, cur, prev, bo)
                        att_finish(ti, bo)
                    barrier("att")

                if STAGE >= 4:
                  with ExitStack() as pg:
                    def gsb(name, shape, dt):
                        return pg.enter_context(nc.sbuf_tensor("g_" + name, list(shape), dt))
                    WGqd = gsb("WGqd", [128, 8, 4, 128], BF16)
                    WGkd = gsb("WGkd", [128, 8, 4, 128], BF16)
                    WGk = gsb("WGk", [128, 8, 256], BF16)
                    WGv = gsb("WGv", [128, 8, 512], BF16)
                    WGr = gsb("WGr", [128, 8, 512], BF16)
                    WGl = gsb("WGl", [128, 8, 16], BF16)

                    def dupdst(Wd):
                        def f(c0, w_):
                            h0 = c0 // 64
                            return [(Wd[:, :, h0 + hh, d0:d0 + 64], hh * 64, hh * 64 + 64)
                                    for hh in range(2) for d0 in (0, 64)]
                        return f
                    load_cols(dupdst(WGqd), 768, 256, 'WGqd')
                    load_cols(dupdst(WGkd), 1024, 256, 'WGkd')
                    load_cols(lambda c0, w_: [(WGk[:, :, c0:c0 + w_], 0, w_)], 1024, 256, 'WGk')
                    load_cols(lambda c0, w_: [(WGv[:, :, c0:c0 + w_], 0, w_)], 1280, 512, 'WGv')
                    load_cols(lambda c0, w_: [(WGr[:, :, c0:c0 + w_], 0, w_)], 1792, 512, 'WGr')
                    load_cols(lambda c0, w_: [(WGl[:, :, 0:16], 0, 16)], 2304, 16, 'WGl')
                    glT = gsb("glT", [16, 128], BF16)
                    ez = gsb("ez", [128, 256], F32)
                    la = gsb("la", [128, 256], F32)
                    lad = gsb("lad", [128, 4, 128], F32)
                    ee1 = gsb("ee1", [128, 256], F32)
                    dec = gsb("dec", [128, 4, 2], F32)
                    kdd = gsb("kdd", [128, 4, 128], BF16)
                    vg = gsb("vg", [128, 512], BF16)
                    S = gsb("S", [128, 512], F32)
                    Y = gsb("Y", [128, 512], BF16)
                    eb = gsb("eb", [128, 512], F32)
                    enb = gsb("enb", [128, 512], F32)
                    tq = gsb("tq", [128, 512], F32)
                    X = gsb("X", [128, 512], BF16)
                    keT = gsb("keT", [128, 512], BF16)
                    aTm = gsb("aTm", [128, 512], BF16)
                    sr = gsb("sr", [128, 512], F32)
                    ss = gsb("ss", [128, 4], F32)
                    yb = gsb("yb", [128, 512], BF16)
                    Lsave = gsb("Lsave", [128, 512], F32)
                    Ptot = gsb("Ptot", [128, 4], F32)
                    EX = gsb("EX", [128, 516], F32)
                    Gr = gsb("Gr", [128, 2, 516], F32)
                    Xs = gsb("Xs", [128, 512], F32)
                    Ep = gsb("Ep", [128, 512], F32)
                    Ap = gsb("Ap", [128, 4], F32)
                    P.add('dve', lambda: nc.vector.memset(S[:], 0.0), w=['S'])

                    def gla_tile(ul, ti, outputs, meta=False, ptot=False):
                        u_ = lambda c: ust[:, c, ul:ul + 128]
                        for c in range(8):
                            P.add('pe', (lambda c=c: nc.tensor.matmul(
                                banks[0][:, 0:256], u_(c), WGk[:, c, :], start=(c == 0), stop=(c == 7))),
                                r=['ust', 'WGk'], w=['bank0'])
                        for c in range(8):
                            P.add('pe', (lambda c=c: nc.tensor.matmul(
                                banks[1][:, 0:512], u_(c), WGv[:, c, :], start=(c == 0), stop=(c == 7))),
                                r=['ust', 'WGv'], w=['bank1'])
                        for c in range(8):
                            P.add('pe', (lambda c=c: nc.tensor.matmul(
                                banks[3][0:16, 256:384], WGl[:, c, :], u_(c), start=(c == 0), stop=(c == 7))),
                                r=['ust', 'WGl'], w=['bank3'])
                        P.add('act', lambda: nc.scalar.copy(out=glT[:], in_=banks[3][0:16, 256:384]),
                              r=['bank3'], w=['glT'])
                        P.add('pe', lambda: nc.tensor.matmul(
                            banks[0][:, 256:512], glT[:], gwb[:], start=True, stop=False),
                            r=['glT', 'gwb'], w=['bank0'])
                        P.add('pe', lambda: nc.tensor.matmul(
                            banks[0][:, 256:512], ones[0:1, 0:128], gbb[:], start=False, stop=True),
                            r=['ones', 'gbb'], w=['bank0'])
                        P.add('act', lambda: nc.scalar.activation(
                            out=ez[:], in_=banks[0][:, 256:512], func=AF.Exp, scale=-1.0),
                            r=['bank0'], w=['ez'])
                        P.add('act', lambda: nc.scalar.activation(
                            out=ez[:], in_=ez[:], func=AF.Ln, bias=C('one'), scale=1.0),
                            r=['ez', 'cst'], w=['ez'])
                        if meta:
                            P.add('dve', lambda: nc.vector.tensor_scalar(
                                out=la[:], in0=ez[:], scalar1=-1.0 / 16.0, scalar2=C('tokm'),
                                op0=ALU.mult, op1=ALU.mult), r=['ez', 'cst'], w=['la'])
                        else:
                            P.add('dve', lambda: nc.vector.tensor_scalar(
                                out=la[:], in0=ez[:], scalar1=-1.0 / 16.0, scalar2=None,
                                op0=ALU.mult), r=['ez'], w=['la'])
                        for d0 in (0, 64):
                            P.add('dve', (lambda d0=d0: nc.vector.tensor_copy(
                                out=lad[:, :, d0:d0 + 64], in_=la[:].rearrange("p (h d) -> p h d", h=4))),
                                r=['la'], w=['lad'])
                        for h in range(4):
                            P.add('pe', (lambda h=h: nc.tensor.matmul(
                                banks[2][:, h * 128:(h + 1) * 128], lad[:, h, :], C('U2'),
                                start=True, stop=True)), r=['lad', 'cst'], w=['bank2'])
                        P.add('pe', lambda: nc.tensor.matmul(
                            banks[3][:, 0:256], C('SU'), la[:], start=True, stop=True),
                            r=['la', 'cst'], w=['bank3'])
                        P.add('act', lambda: nc.scalar.activation(
                            out=ee1[:], in_=banks[3][:, 0:256], func=AF.Exp), r=['bank3'], w=['ee1'])
                        P.add('act', lambda: nc.scalar.activation(
                            out=dec[:], in_=banks[2][:, :].rearrange("p (h t) -> p h t", h=4)[:, :, 63::64],
                            func=AF.Exp), r=['bank2'], w=['dec'])
                        if outputs:
                            P.add('act', lambda: nc.scalar.activation(
                                out=eb[:], in_=banks[2][:, :], func=AF.Exp), r=['bank2'], w=['eb'])
                            P.add('act', lambda: nc.scalar.activation(
                                out=enb[:], in_=banks[2][:, :], func=AF.Exp, scale=-1.0),
                                r=['bank2'], w=['enb'])
                        for d0 in (0, 64):
                            P.add('dve', (lambda d0=d0: nc.vector.tensor_tensor(
                                out=kdd[:, :, d0:d0 + 64],
                                in0=banks[0][:, 0:256].rearrange("p (h d) -> p h d", h=4),
                                in1=ee1[:].rearrange("p (h d) -> p h d", h=4), op=ALU.mult)),
                                r=['bank0', 'ee1'], w=['kdd'])
                        P.add('act', lambda: nc.scalar.copy(out=vg[:], in_=banks[1][:, :]),
                              r=['bank1'], w=['vg'])
                        for (bk, lo) in ((4, 0), (5, 64)):
                            for h in range(4):
                                P.add('pe', (lambda bk=bk, lo=lo, h=h: nc.tensor.matmul(
                                    banks[bk][:, h * 128:(h + 1) * 128], kdd[lo:lo + 64, h, :],
                                    vg[lo:lo + 64, h * 128:(h + 1) * 128], start=True, stop=True)),
                                    r=['kdd', 'vg'], w=['bank%d' % bk])
                        if outputs:
                            P.add('dve', lambda: nc.vector.tensor_copy(out=Y[0:64, :], in_=S[0:64, :]),
                                  r=['S'], w=['Y'])
                        for h in range(4):
                            P.add('dve', (lambda h=h: nc.vector.scalar_tensor_tensor(
                                out=S[:, h * 128:(h + 1) * 128], in0=S[:, h * 128:(h + 1) * 128],
                                scalar=dec[:, h, 0:1], in1=banks[4][:, h * 128:(h + 1) * 128],
                                op0=ALU.mult, op1=ALU.add)), r=['S', 'dec', 'bank4'], w=['S'])
                        if outputs:
                            P.add('dve', lambda: nc.vector.tensor_copy(out=Y[64:128, :], in_=S[64:128, :]),
                                  r=['S'], w=['Y'])
                        for h in range(4):
                            P.add('dve', (lambda h=h: nc.vector.scalar_tensor_tensor(
                                out=S[:, h * 128:(h + 1) * 128], in0=S[:, h * 128:(h + 1) * 128],
                                scalar=dec[:, h, 1:2], in1=banks[5][:, h * 128:(h + 1) * 128],
                                op0=ALU.mult, op1=ALU.add)), r=['S', 'dec', 'bank5'], w=['S'])
                        if ptot:
                            for j in (0, 1):
                                P.add('dve', (lambda j=j: nc.vector.tensor_tensor(
                                    out=Ptot[:], in0=Ptot[:], in1=dec[:, :, j], op=ALU.mult)),
                                    r=['Ptot', 'dec'], w=['Ptot'])
                        if not outputs:
                            return
                        for (bk, Wd, key) in ((4, WGqd, 'WGqd'), (5, WGkd, 'WGkd')):
                            for h in range(4):
                                for c in range(8):
                                    P.add('pe', (lambda bk=bk, Wd=Wd, h=h, c=c: nc.tensor.matmul(
                                        banks[bk][:, h * 128:(h + 1) * 128], Wd[:, c, h, :], u_(c),
                                        start=(c == 0), stop=(c == 7))), r=['ust', key], w=['bank%d' % bk])
                        P.add('dve', lambda: nc.vector.scalar_tensor_tensor(
                            out=tq[:], in0=banks[4][:, :], scalar=0.125, in1=eb[:],
                            op0=ALU.mult, op1=ALU.mult), r=['bank4', 'eb'], w=['tq'])
                        P.add('dve', lambda: nc.vector.tensor_tensor(
                            out=X[:], in0=tq[:], in1=C('bmask'), op=ALU.mult), r=['tq', 'cst'], w=['X'])
                        P.add('dve', lambda: nc.vector.tensor_tensor(
                            out=keT[:], in0=banks[5][:, :], in1=enb[:], op=ALU.mult),
                            r=['bank5', 'enb'], w=['keT'])
                        for h in range(4):
                            P.add('pe', (lambda h=h: nc.tensor.matmul(
                                banks[0][:, h * 128:(h + 1) * 128], keT[:, h * 128:(h + 1) * 128],
                                X[:, h * 128:(h + 1) * 128], start=True, stop=True)),
                                r=['keT', 'X'], w=['bank0'])
                        P.add('dve', lambda: nc.vector.tensor_tensor(
                            out=aTm[:], in0=banks[0][:, :], in1=C('tmask'), op=ALU.mult),
                            r=['bank0', 'cst'], w=['aTm'])
                        for c in range(8):
                            P.add('pe', (lambda c=c: nc.tensor.matmul(
                                banks[1][:, 0:512], u_(c), WGr[:, c, :], start=(c == 0), stop=(c == 7))),
                                r=['ust', 'WGr'], w=['bank1'])
                        P.add('act', lambda: nc.scalar.activation(out=sr[:], in_=banks[1][:, :], func=AF.Silu),
                              r=['bank1'], w=['sr'])
                        for h in range(4):
                            hs = slice(h * 128, (h + 1) * 128)
                            P.add('pe', (lambda hs=hs: nc.tensor.matmul(
                                banks[1][:, hs], aTm[:, hs], vg[:, hs], start=True, stop=False)),
                                r=['aTm', 'vg'], w=['bank1'])
                            P.add('pe', (lambda hs=hs: nc.tensor.matmul(
                                banks[1][:, hs], X[:, hs], Y[:, hs], start=False, stop=True)),
                                r=['X', 'Y'], w=['bank1'])
                        P.add('act', lambda: nc.scalar.activation(out=tq[:], in_=banks[1][:, :], func=AF.Square),
                              r=['bank1'], w=['tq'])
                        P.add('dve', lambda: nc.vector.tensor_reduce(
                            out=ss[:], in_=tq[:].rearrange("p (h v) -> p h v", h=4), axis=AX.X, op=ALU.add),
                            r=['tq'], w=['ss'])
                        P.add('act', lambda: nc.scalar.activation(
                            out=ss[:], in_=ss[:], func=AF.Sqrt, bias=epsb[:, 0:1], scale=1.0 / 128.0),
                            r=['ss', 'epsb'], w=['ss'])
                        P.add('dve', lambda: nc.vector.reciprocal(out=ss[:], in_=ss[:]), r=['ss'], w=['ss'])
                        for h in range(4):
                            hs = slice(h * 128, (h + 1) * 128)
                            P.add('dve', (lambda hs=hs, h=h: nc.vector.scalar_tensor_tensor(
                                out=eb[:, hs], in0=banks[1][:, hs], scalar=ss[:, h:h + 1], in1=C('gn'),
                                op0=ALU.mult, op1=ALU.mult)), r=['bank1', 'ss', 'cst'], w=['eb'])
                        P.add('dve', lambda: nc.vector.tensor_tensor(
                            out=yb[:], in0=eb[:], in1=sr[:], op=ALU.mult), r=['eb', 'sr'], w=['yb'])
                        for k4 in range(4):
                            P.add('pe', (lambda k4=k4: nc.tensor.transpose(
                                ptb[:, k4 * 128:(k4 + 1) * 128], yb[:, k4 * 128:(k4 + 1) * 128], ident[:])),
                                r=['yb', 'ident'], w=['bankT'])
                        P.add('act', (lambda ti=ti: nc.scalar.copy(
                            out=ybT[:, :, ti * 128:(ti + 1) * 128],
                            in_=ptb[:, 0:512].rearrange("p (a b) -> p a b", a=4))),
                            r=['bankT'], w=['ybT%d' % ti])

                    norm_pass([(0, 128)], 1, sq, rs, "ng0", ust, 0, ust_keys)
                    gla_tile(0, -1, False, meta=True)
                    P.add('dve', lambda: nc.vector.tensor_scalar(
                        out=S[:], in0=S[:], scalar1=C('fmeta'), scalar2=None, op0=ALU.mult),
                        r=['S', 'cst'], w=['S'])
                    if not os.environ.get('KNOX'):
                        P.add('dve', lambda: nc.vector.tensor_copy(out=Lsave[:], in_=S[:]), r=['S'], w=['Lsave'])
                        P.add('dve', lambda: nc.vector.memset(Ptot[:], 1.0), w=['Ptot'])
                        for sti, (t0, nn) in enumerate(supertiles(OWN0, T_ALL)):
                            norm_pass([(t0, nn)], 1, sq, rs, "ng1", ust, t0, ust_keys)
                            for j in range(4):
                                gla_tile(j * 128, sti * 4 + j, False, ptot=True)
                        P.add('dve', lambda: nc.vector.tensor_copy(out=EX[:, 0:512], in_=S[:]), r=['S'], w=['EX'])
                        P.add('dve', lambda: nc.vector.tensor_copy(out=EX[:, 512:516], in_=Ptot[:]),
                              r=['Ptot'], w=['EX'])
                        P.add('sp', lambda: nc.sync.dma_start(out=cc_in.ap(), in_=EX[:]), r=['EX'], w=['cc_in'], dma=True)
                        P.add('pool', lambda: nc.gpsimd.collective_compute(
                            "AllGather", ALU.bypass, replica_groups=[list(range(N_CORES))],
                            ins=[cc_in.ap().opt()], outs=[cc_out.ap().opt()]),
                            r=['cc_in'], w=['cc_out'], dma='cc')
                        P.add('dve', lambda: nc.vector.memset(Xs[:], 0.0), w=['Xs'])
                        for r_ in range(N_CORES):
                            q = r_ % 2
                            P.add('sp', (lambda r_=r_, q=q: nc.sync.dma_start(
                                out=Gr[:, q, :], in_=cc_out.ap()[r_ * 128:(r_ + 1) * 128, :])),
                                r=['cc_out'], w=['Gr%d' % q], dma=True)
                            P.add('dve', (lambda r_=r_, q=q: nc.vector.tensor_scalar(
                                out=Ap[:], in0=Gr[:, q, 512:516], scalar1=C('sel')[:, r_:r_ + 1],
                                scalar2=C('nsel')[:, r_:r_ + 1], op0=ALU.mult, op1=ALU.add)),
                                r=['Gr%d' % q, 'cst'], w=['Ap'])
                            P.add('dve', (lambda r_=r_, q=q: nc.vector.tensor_scalar(
                                out=Ep[:], in0=Gr[:, q, 0:512], scalar1=C('sel')[:, r_:r_ + 1],
                                scalar2=None, op0=ALU.mult)), r=['Gr%d' % q, 'cst'], w=['Ep'])
                            for h in range(4):
                                hs = slice(h * 128, (h + 1) * 128)
                                P.add('dve', (lambda hs=hs, h=h: nc.vector.scalar_tensor_tensor(
                                    out=Xs[:, hs], in0=Xs[:, hs], scalar=Ap[:, h:h + 1], in1=Ep[:, hs],
                                    op0=ALU.mult, op1=ALU.add)), r=['Xs', 'Ap', 'Ep'], w=['Xs'])
                        P.add('dve', lambda: nc.vector.tensor_tensor(
                            out=S[:], in0=Lsave[:], in1=Xs[:], op=ALU.add), r=['Lsave', 'Xs'], w=['S'])
                    for sti, (t0, nn) in enumerate(supertiles(OWN0, T_ALL)):
                        norm_pass([(t0, nn)], 1, sq, rs, "ng", ust, t0, ust_keys)
                        for j in range(4):
                            gla_tile(j * 128, sti * 4 + j, True)
                    barrier("gla")
                if STAGE in (3, 4):
                    for k4 in range(4):
                        for (t0, nn) in supertiles(0, T_OWN):
                            P.add('dve', (lambda k4=k4, t0=t0, nn=nn: nc.vector.tensor_copy(
                                out=hT[:, k4, OWN0 + t0:OWN0 + t0 + nn], in_=yaT[:, k4, t0:t0 + nn])),
                                r=['yaT%d' % ti for ti in range(t0 // 128, (t0 + nn) // 128)],
                                w=hkeys(k4, OWN0 + t0, nn))
                    if STAGE == 4:
                        for k4 in range(4):
                            for (t0, nn) in supertiles(0, T_OWN):
                                P.add('dve', (lambda k4=k4, t0=t0, nn=nn: nc.vector.tensor_copy(
                                    out=hT[:, 4 + k4, OWN0 + t0:OWN0 + t0 + nn], in_=ybT[:, k4, t0:t0 + nn])),
                                    r=['ybT%d' % ti for ti in range(t0 // 128, (t0 + nn) // 128)],
                                    w=hkeys(4 + k4, OWN0 + t0, nn))

                pu.close()
                if STAGE >= 5:
                  with ExitStack() as pb:
                    def bsb(name, shape, dt):
                        return pb.enter_context(nc.sbuf_tensor("b_" + name, list(shape), dt))
                    uown = bsb("uown", [128, 8, T_OWN], BF16)
                    mixT = bsb("mixT", [128, 8, T_OWN], BF16)
                    wga = bsb("wga", [128, 1, 8, 128], BF16)
                    wgb = bsb("wgb", [128, 1, 8, 128], BF16)
                    wab = bsb("wab", [128, 1, 8, 128], BF16)
                    wo = bsb("wo", [128, 1, 8, 128], BF16)
                    sga = bsb("sga", [128, 2, 512], F32)
                    t1 = bsb("t1", [128, 512], F32)
                    t2 = bsb("t2", [128, 512], F32)
                    own_sts = supertiles(OWN0, T_ALL)

                    def ukeys_own(c, t0, nn):
                        return ['uo%d.%d' % (c, tt) for tt in range(t0 // 128, (t0 + nn) // 128)]
                    norm_pass(own_sts, 1, sq, rs, "nb", uown, OWN0, ukeys_own)

                    def mkeys(m, t0, nn):
                        return ['mx%d.%d' % (m, tt) for tt in range(t0 // 128, (t0 + nn) // 128)]
                    for m in range(8):
                        sl = 0
                        load_cols(lambda c0, w_, sl=sl: [(wga[:, sl, :, :], 0, 128)], 2320 + m * 128, 128, 'wga%d' % sl)
                        load_cols(lambda c0, w_, sl=sl: [(wgb[:, sl, :, :], 0, 128)], 3344 + m * 128, 128, 'wgb%d' % sl)
                        load_cols(lambda c0, w_, sl=sl: [(wab[:, sl, :, :], 0, 128)], m * 128, 128, 'wab%d' % sl, src=wab_d)
                        for (t0, nn) in own_sts:
                            o0 = t0 - OWN0
                            for br, (wg_, gk) in enumerate(((wga, 'wga%d' % sl), (wgb, 'wgb%d' % sl))):
                                yT, ykey = (yaT, 'yaT') if br == 0 else (ybT, 'ybT')
                                bA = nb()
                                for k4 in range(4):
                                    P.add('pe', (lambda bA=bA, k4=k4, br=br, yT=yT, o0=o0, nn=nn, sl=sl: nc.tensor.matmul(
                                        banks[bA][:, :nn], wab[:, sl, br * 4 + k4, :], yT[:, k4, o0:o0 + nn],
                                        start=(k4 == 0), stop=(k4 == 3))),
                                        r=['wab%d' % sl] + ['%s%d' % (ykey, tt) for tt in range(o0 // 128, (o0 + nn) // 128)],
                                        w=[BK(bA)])
                                bG = nb()
                                for c in range(8):
                                    P.add('pe', (lambda bG=bG, c=c, wg_=wg_, t0=t0, nn=nn, sl=sl: nc.tensor.matmul(
                                        banks[bG][:, :nn], wg_[:, sl, c, :], uown[:, c, t0 - OWN0:t0 - OWN0 + nn],
                                        start=(c == 0), stop=(c == 7))),
                                        r=[gk] + ukeys_own(c, t0, nn), w=[BK(bG)])
                                P.add('act', (lambda bG=bG, br=br, nn=nn: nc.scalar.activation(
                                    out=sga[:, br, :nn], in_=banks[bG][:, :nn], func=AF.Sigmoid)),
                                    r=[BK(bG)], w=['sga%d' % br])
                                tt_ = t1 if br == 0 else t2
                                P.add('dve', (lambda bA=bA, br=br, tt_=tt_, nn=nn: nc.vector.tensor_tensor(
                                    out=tt_[:, :nn], in0=sga[:, br, :nn], in1=banks[bA][:, :nn], op=ALU.mult)),
                                    r=['sga%d' % br, BK(bA)], w=['t%d' % (br + 1)])
                            P.add('dve', (lambda m=m, o0=o0, nn=nn: nc.vector.tensor_tensor(
                                out=mixT[:, m, o0:o0 + nn], in0=t1[:, :nn], in1=t2[:, :nn], op=ALU.add)),
                                r=['t1', 't2'], w=mkeys(m, o0, nn))
                    for m in range(8):
                        sl = 0
                        load_cols(lambda c0, w_, sl=sl: [(wo[:, sl, :, :], 0, 128)], m * 128, 128, 'wo%d' % sl, src=wout_d)
                        for (t0, nn) in own_sts:
                            o0 = t0 - OWN0
                            bO = nb()
                            for c in range(8):
                                P.add('pe', (lambda bO=bO, c=c, o0=o0, nn=nn, sl=sl: nc.tensor.matmul(
                                    banks[bO][:, :nn], wo[:, sl, c, :], mixT[:, c, o0:o0 + nn],
                                    start=(c == 0), stop=(c == 7))),
                                    r=['wo%d' % sl] + mkeys(c, o0, nn), w=[BK(bO)])
                            P.add('dve', (lambda bO=bO, m=m, t0=t0, nn=nn: nc.vector.tensor_tensor(
                                out=hT[:, m, t0:t0 + nn], in0=banks[bO][:, :nn], in1=hT[:, m, t0:t0 + nn],
                                op=ALU.add)), r=[BK(bO)] + hkeys(m, t0, nn), w=hkeys(m, t0, nn))
                barrier("mix")

            if STAGE >= 2 and not os.environ.get('KNOF2'):
                with ExitStack() as p2:
                    uT = p2.enter_context(nc.sbuf_tensor("uT2", [128, 8, T_ALL], BF16))
                    norm_pass(supertiles(OWN0, T_ALL), 2, sq, rs, "n2", uT)
                    ffn(supertiles(OWN0, T_ALL), ffn_w_d[1], "f2", uT)

            outb = ns.enter_context(nc.sbuf_tensor("outb", [128, 2, 8, 512], F32))
            oc = 0
            for (t0, nn) in supertiles(OWN0, T_ALL):
                bS = 6
                o = oc % 2
                oc += 1
                for c in range(8):
                    q = c % 2
                    P.add('act', (lambda c=c, q=q, t0=t0, nn=nn: nc.scalar.activation(
                        out=sq[:, q, :nn], in_=hT[:, c, t0:t0 + nn], func=AF.Square)),
                        r=hkeys(c, t0, nn), w=['sq%d' % q])
                    P.add('pe', (lambda c=c, q=q, nn=nn: nc.tensor.matmul(
                        banks[bS][:, :nn], ones[:], sq[:, q, :nn], start=(c == 0), stop=(c == 7))),
                        r=['ones', 'sq%d' % q], w=['bank%d' % bS])
                P.add('act', (lambda nn=nn: nc.scalar.activation(
                    out=rs[:, :nn], in_=banks[bS][:, :nn], func=AF.Sqrt, bias=epsb[:, 0:1],
                    scale=1.0 / D)), r=['bank%d' % bS, 'epsb'], w=['rs'])
                P.add('dve', (lambda nn=nn: nc.vector.reciprocal(
                    out=rs[:, :nn], in_=rs[:, :nn])), r=['rs'], w=['rs'])
                for c in range(8):
                    P.add('dve', (lambda c=c, o=o, t0=t0, nn=nn: nc.vector.scalar_tensor_tensor(
                        out=outb[:, o, c, :nn], in0=hT[:, c, t0:t0 + nn],
                        scalar=norms[:, 3, c:c + 1], in1=rs[:, :nn],
                        op0=ALU.mult, op1=ALU.mult)),
                        r=hkeys(c, t0, nn) + ['rs', 'norms'], w=['outb%d' % o])
                P.add('sp', (lambda o=o, t0=t0, nn=nn: nc.sync.dma_start(
                    out=out_d[:, :, t0 - OWN0:t0 - OWN0 + nn], in_=outb[:, o, :, :nn])),
                    r=['outb%d' % o], w=['OUT%d' % t0], dma=True)
            P.add('sp', lambda: nc.sync.nop(), r=['OUT%d' % t0 for (t0, nn) in supertiles(OWN0, T_ALL)])
            P.emit()
    return nc, P


def _fm(a):
    T = a.shape[0]
    return np.ascontiguousarray(a.reshape(T, 8, 128).transpose(2, 1, 0))


def _ffn_layout(wg, wu, wd):
    def gl(w):
        a = w.reshape(8, 128, NG, G, 128).transpose(2, 1, 3, 0, 4)
        return np.ascontiguousarray(a).reshape(NG, 128, G * 1024)
    d = wd.reshape(NG, G, 128, 1024).transpose(0, 2, 1, 3)
    return gl(wg), gl(wu), np.ascontiguousarray(d).reshape(NG, 128, G * 1024)


_CACHE = {}


def kernel(x, meta_tokens, ffn1_norm, ffn1_w_gate, ffn1_w_up, ffn1_w_down, mix_norm, w_in,
           gla_gate_w, gla_gate_b, gla_out_norm, att_sinks, w_branch_att, w_branch_gla, w_out,
           ffn2_norm, ffn2_w_gate, ffn2_w_up, ffn2_w_down, final_norm):
    f32 = np.float32
    x = np.asarray(x, f32)
    if 'nc' not in _CACHE:
        _CACHE['nc'] = build_nc()
    nc, P = _CACHE['nc']

    shared = {}
    nrm = np.stack([np.asarray(ffn1_norm, f32)[0], np.asarray(mix_norm, f32)[0],
                    np.asarray(ffn2_norm, f32)[0], np.asarray(final_norm, f32)], 0)
    shared["norms"] = np.ascontiguousarray(nrm.reshape(4, 8, 128).transpose(2, 0, 1))
    for l, (wg, wu, wd) in enumerate(((ffn1_w_gate, ffn1_w_up, ffn1_w_down),
                                      (ffn2_w_gate, ffn2_w_up, ffn2_w_down)), 1):
        a, b, c = _ffn_layout(np.asarray(wg, f32)[0], np.asarray(wu, f32)[0], np.asarray(wd, f32)[0])
        shared["f%d_wg" % l], shared["f%d_wu" % l], shared["f%d_wd" % l] = a, b, c

    W_in = np.asarray(w_in, f32)[0]
    shared["win"] = np.ascontiguousarray(W_in.reshape(8, 128, 4368).transpose(1, 0, 2))
    wa = np.asarray(w_branch_att, f32)[0].reshape(4, 128, 1024).transpose(1, 0, 2)
    wb = np.asarray(w_branch_gla, f32)[0].reshape(4, 128, 1024).transpose(1, 0, 2)
    shared["wab"] = np.ascontiguousarray(np.concatenate([wa, wb], 1))
    shared["wo"] = np.ascontiguousarray(np.asarray(w_out, f32)[0].reshape(8, 128, 1024).transpose(1, 0, 2))
    shared["gw"] = np.ascontiguousarray(np.concatenate(
        [np.asarray(gla_gate_w, f32)[0], np.asarray(gla_gate_b, f32)[0][None]], 0))
    BIG = 1.0e7
    p = np.arange(128)
    dist = np.zeros((128, 272), f32)
    kpos = np.arange(256) - 128
    dd = np.abs(p[:, None] - kpos[None, :]).astype(f32)
    qc = p[:, None] // 64
    kc = np.floor_divide(kpos[None, :], 64)
    valid = (kc <= qc) & (kc >= qc - 2)
    dist[:, :256] = np.where(valid, dd, BIG)
    dist1 = dist.copy()
    dist1[:, :128] = BIG
    same = (p[:, None] // 64) == (p[None, :] // 64)
    U2 = (same & (p[:, None] <= p[None, :])).astype(f32)
    SU = (same & (p[:, None] > p[None, :])).astype(f32)
    bm = ((p[:, None] // 64) == (p[None, :] // 64)).astype(f32)
    cst = np.zeros((128, CW), f32)

    def put(name, a):
        o, w_ = COFF[name]
        cst[:, o:o + w_] = a
    put('dist', dist)
    put('ident', np.eye(128, dtype=f32))
    put('U2', U2)
    put('SU', SU)
    put('bmask', np.tile(bm, (1, 4)))
    put('tmask', np.tile(U2, (1, 4)))
    put('sinkb', np.broadcast_to(np.asarray(att_sinks, f32)[0][None, :], (128, 8)))
    put('gn', np.broadcast_to(np.asarray(gla_out_norm, f32)[0][None, :], (128, 128)))
    put('tokm', (p < 16).astype(f32)[:, None])
    put('one', np.ones((128, 1), f32))

    meta = np.asarray(meta_tokens, f32)
    in_maps = []
    for core in range(N_CORES):
        b, s = divmod(core, 4)
        tok = np.zeros((T_ALL, D), f32)
        tok[0:16] = meta
        if s > 0:
            tok[128:256] = x[b, s * T_OWN - 128: s * T_OWN]
        tok[256:] = x[b, s * T_OWN:(s + 1) * T_OWN]
        m = dict(shared)
        m["xT"] = _fm(tok)
        cc = cst.copy()
        o, w_ = COFF['dist1']
        cc[:, o:o + w_] = dist1 if s == 0 else dist
        cc[:, COFF['fmeta'][0]] = 1.0 if s == 0 else 0.0
        for r in range(N_CORES):
            sv_ = 1.0 if (r // 4 == b and r < core) else 0.0
            cc[:, COFF['sel'][0] + r] = sv_
            cc[:, COFF['nsel'][0] + r] = 1.0 - sv_
        m["cst"] = cc
        in_maps.append(m)

    res = run_bass_kernel_spmd(nc, in_maps, core_ids=list(range(N_CORES)))
    out = np.empty((2, 8192, D), f32)
    for core in range(N_CORES):
        b, s = divmod(core, 4)
        oT = np.asarray(res.results[core]["outT"], f32)
        out[b, s * T_OWN:(s + 1) * T_OWN] = oT.transpose(2, 1, 0).reshape(T_OWN, D)
    return out
```

```python
import numpy as np
import concourse.bass as bass
import concourse.mybir as mybir
from concourse.bass_utils import run_bass_kernel_spmd

F32 = mybir.dt.float32
BF16 = mybir.dt.bfloat16
AF = mybir.ActivationFunctionType
ALU = mybir.AluOpType
AX = mybir.AxisListType

D = 1024
DFF = 2816
NF = DFF // 128
G = 2
NG = NF // G
T_ALL = 2304
T_OWN = 2048
OWN0 = 256
EPS = 1e-6
N_CORES = 8
ENGS = ['pe', 'act', 'dve', 'sp']

import os
STAGE = int(os.environ.get('KSTAGE', '99'))


class Prog:
    NS = 24

    def __init__(self, nc):
        self.nc = nc
        self.eng = {'pe': nc.tensor, 'act': nc.scalar, 'dve': nc.vector,
                    'pool': nc.gpsimd, 'sp': nc.sync}
        self.ops = []

    def add(self, eng, fn, r=(), w=(), dma=False):
        self.ops.append((eng, fn, tuple(r), tuple(w), dma))

    def emit(self):
        nc = self.nc
        ops = self.ops
        n = len(ops)
        last_w = {}
        readers = {}
        AENG = ENGS + ['pool']
        eng_n = {e: 0 for e in AENG}
        op_local = [0] * n
        clock = {e: {} for e in AENG}
        op_clock = [None] * n
        waits = [None] * n
        needed = set()
        dma_sem_of = {}
        dma_cnt = [0] * (self.NS + 1)
        ndma = 0
        for i, (e, fn, r, w, dma) in enumerate(ops):
            raw = set()
            deps = set()
            for k in r:
                j = last_w.get(k)
                if j is not None:
                    deps.add(j)
                    raw.add(j)
            for k in w:
                j = last_w.get(k)
                if j is not None:
                    deps.add(j)
                for j in readers.get(k, ()):
                    deps.add(j)
            deps.discard(i)
            ck = clock[e]
            li = eng_n[e]
            my_w = []
            for j in sorted(deps, reverse=True):
                ej, _, _, _, dmaj = ops[j]
                if dmaj:
                    s, v = dma_sem_of[j]
                    if ck.get(('d', s), 0) >= v:
                        continue
                    my_w.append(('d', s, v))
                else:
                    lj = op_local[j]
                    if ej == e:
                        if e == 'pe' or dma:
                            continue
                        if j not in raw or li - lj > 3:
                            continue
                        if ck.get(('self', e), -1) >= lj:
                            continue
                        my_w.append(('e', ej, j))
                        needed.add(j)
                        ck[('self', e)] = lj
                        continue
                    if ck.get(ej, -1) >= lj:
                        continue
                    my_w.append(('e', ej, j))
                    needed.add(j)
                for kk, vv in op_clock[j].items():
                    if ck.get(kk, -1) < vv:
                        ck[kk] = vv
            waits[i] = my_w
            oc = {kk: vv for kk, vv in ck.items() if not (isinstance(kk, tuple) and kk[0] == 'self')}
            if dma:
                if dma == 'cc':
                    s = self.NS
                    dma_cnt[s] += 1
                else:
                    s = ndma % self.NS
                    ndma += 1
                    dma_cnt[s] += 16
                dma_sem_of[i] = (s, dma_cnt[s])
                oc[('d', s)] = dma_cnt[s]
            else:
                op_local[i] = li
                eng_n[e] = li + 1
                oc[e] = li
            op_clock[i] = oc
            for k in r:
                readers.setdefault(k, []).append(i)
            for k in w:
                last_w[k] = i
                readers[k] = []
        del op_clock
        sems = {e: nc.alloc_semaphore(name='s_' + e) for e in AENG}
        dsems = [nc.alloc_semaphore(name='d%d' % s) for s in range(self.NS + 1)]
        sig = {e: 0 for e in AENG}
        sigval = {}
        for i, (e, fn, r, w, dma) in enumerate(ops):
            if i in needed:
                sig[e] += 1
                sigval[i] = sig[e]
        if os.environ.get('KSIM'):
            q = {e: [i for i in range(n) if ops[i][0] == e] for e in AENG}
            pos = {e: 0 for e in AENG}
            sv = {e: 0 for e in AENG}
            dv = [0] * (self.NS + 1)
            prog = True
            while prog:
                prog = False
                for e in AENG:
                    while pos[e] < len(q[e]):
                        i = q[e][pos[e]]
                        ok = True
                        for wt in waits[i]:
                            if wt[0] == 'd':
                                ok = ok and dv[wt[1]] >= wt[2]
                            else:
                                ok = ok and sv[wt[1]] >= sigval[wt[2]]
                        if not ok:
                            break
                        if ops[i][4]:
                            dv[dma_sem_of[i][0]] += (1 if ops[i][4] == 'cc' else 16)
                        elif i in needed:
                            sv[e] += 1
                        pos[e] += 1
                        prog = True
            for e in AENG:
                if pos[e] < len(q[e]):
                    i = q[e][pos[e]]
                    print("DEADLOCK", e, pos[e], len(q[e]), i, ops[i][2], ops[i][3], waits[i],
                          [(wt, sigval.get(wt[2])) for wt in waits[i] if wt[0] == 'e'], sv, dv)
            print("SIM done", pos)
        for i, (e, fn, r, w, dma) in enumerate(ops):
            E = self.eng[e]
            for wt in waits[i]:
                if wt[0] == 'd':
                    E.wait_ge(dsems[wt[1]], wt[2])
                else:
                    E.wait_ge(sems[wt[1]], sigval[wt[2]])
            step = 1 if dma == 'cc' else 16
            if dma and dma_sem_of[i][1] > step:
                E.wait_ge(dsems[dma_sem_of[i][0]], dma_sem_of[i][1] - step)
            ins = fn()
            if dma:
                ins.then_inc(dsems[dma_sem_of[i][0]], step)
            elif i in needed:
                ins.then_inc(sems[e], 1)
        self.stats = (n, {e: eng_n[e] for e in AENG}, ndma, len(needed))
        self.sigstats = (dict(sig), max(dma_cnt))


COFF = {}
_o = 0
for _n, _w in (('dist', 272), ('dist1', 272), ('ident', 128), ('U2', 128), ('SU', 128),
               ('bmask', 512), ('tmask', 512), ('sinkb', 8), ('gn', 128), ('tokm', 1), ('one', 1), ('fmeta', 1), ('sel', 8), ('nsel', 8)):
    COFF[_n] = (_o, _w)
    _o += _w
CW = _o


def supertiles(t0, t1, size=512, split_last=False):
    out = []
    t = t0
    while t < t1:
        nn = min(size, t1 - t)
        out.append((t, nn))
        t += nn
    if split_last and out[-1][1] == size:
        a, nn = out.pop()
        out.append((a, nn // 2))
        out.append((a + nn // 2, nn // 2))
    return out


def build_nc():
    nc = bass.Bass("TRN2", target_bir_lowering=False)
    P = Prog(nc)

    def din(name, shape):
        return nc.dram_tensor(name, list(shape), F32, kind="ExternalInput").ap()

    xT_d = din("xT", [128, 8, T_ALL])
    norms_d = din("norms", [128, 4, 8])
    ffn_w_d = []
    for l in (1, 2):
        ffn_w_d.append((din("f%d_wg" % l, [NG, 128, G * 1024]),
                        din("f%d_wu" % l, [NG, 128, G * 1024]),
                        din("f%d_wd" % l, [NG, 128, G * 1024])))
    out_d = nc.dram_tensor("outT", [128, 8, T_OWN], F32, kind="ExternalOutput").ap()

    from contextlib import ExitStack
    with ExitStack() as es:
        def sb(name, shape, dt):
            return es.enter_context(nc.sbuf_tensor("sb_" + name, list(shape), dt))

        def ps(name, shape, dt=F32):
            return es.enter_context(nc.psum_tensor(name, list(shape), dt))

        hT = sb("hT", [128, 8, T_ALL], F32)
        norms = sb("norms", [128, 4, 8], F32)
        ones = sb("ones", [128, 128], BF16)
        banks = [ps("bank%d" % i, [128, 512]) for i in range(7)]
        ptb = ps("bankT", [128, 1024], BF16)

        epsb = sb("epsb", [128, 1], F32)
        P.add('dve', lambda: nc.vector.memset(ones[:], 1.0), w=['ones'])
        P.add('dve', lambda: nc.vector.memset(epsb[:], EPS), w=['epsb'])
        P.add('sp', lambda: nc.sync.dma_start(out=norms[:], in_=norms_d), w=['norms'], dma=True)
        for c in range(8):
            for (t0, nn) in supertiles(0, T_ALL, 1152):
                P.add('sp', (lambda c=c, t0=t0, nn=nn: nc.sync.dma_start(
                    out=hT[:, c, t0:t0 + nn], in_=xT_d[:, c, t0:t0 + nn])),
                    w=['h%d.%d' % (c, tt) for tt in range(t0 // 128, (t0 + nn) // 128)], dma=True)

        def hkeys(c, t0, nn):
            return ['h%d.%d' % (c, tt) for tt in range(t0 // 128, (t0 + nn) // 128)]

        def ukeys_default(c, t0, nn):
            return ['u%d.%d' % (c, tt) for tt in range(t0 // 128, (t0 + nn) // 128)]

        def norm_pass(sts, gi, sq, rs, tag, uT, uoff=0, ukeys=None):
            ukeys = ukeys or ukeys_default
            cnt = [0]
            for (t0, nn) in sts:
                bS = 6
                for c in range(8):
                    q = cnt[0] % 2
                    cnt[0] += 1
                    P.add('act', (lambda c=c, q=q, t0=t0, nn=nn: nc.scalar.activation(
                        out=sq[:, q, :nn], in_=hT[:, c, t0:t0 + nn], func=AF.Square)),
                        r=hkeys(c, t0, nn), w=['sq%d' % q])
                    P.add('pe', (lambda c=c, q=q, nn=nn: nc.tensor.matmul(
                        banks[bS][:, :nn], ones[:], sq[:, q, :nn], start=(c == 0), stop=(c == 7))),
                        r=['ones', 'sq%d' % q], w=['bank%d' % bS])
                P.add('act', (lambda nn=nn: nc.scalar.activation(
                    out=rs[:, :nn], in_=banks[bS][:, :nn], func=AF.Sqrt, bias=epsb[:, 0:1],
                    scale=1.0 / D)), r=['bank%d' % bS, 'epsb'], w=['rs'])
                P.add('dve', (lambda nn=nn: nc.vector.reciprocal(
                    out=rs[:, :nn], in_=rs[:, :nn])), r=['rs'], w=['rs'])
                for c in range(8):
                    P.add('dve', (lambda c=c, t0=t0, nn=nn: nc.vector.scalar_tensor_tensor(
                        out=uT[:, c, t0 - uoff:t0 - uoff + nn], in0=hT[:, c, t0:t0 + nn],
                        scalar=norms[:, gi, c:c + 1], in1=rs[:, :nn],
                        op0=ALU.mult, op1=ALU.mult)),
                        r=hkeys(c, t0, nn) + ['rs', 'norms'], w=ukeys(c, t0, nn))

        def ffn(sts, wd3, tag, uT):
            ukeys = ukeys_default
            wg_d, wu_d, wd_d = wd3
            with ExitStack() as fs:
                def fsb(name, shape, dt):
                    return fs.enter_context(nc.sbuf_tensor(tag + name, list(shape), dt))
                stg = [fsb("stg%d" % k, [128, G * 1024], F32) for k in range(3)]
                wbf = [[fsb("wbf%d_%d" % (s, k), [128, G * 1024], BF16) for k in range(3)]
                       for s in range(2)]
                sg = fsb("sg", [128, 2, 512], F32)
                act = fsb("act", [128, 2 * G, 512], BF16)
                K = lambda s: tag + s
                gu_cnt = [0]

                def load_group(gi):
                    s = gi % 2
                    for k, wdr in enumerate((wg_d, wu_d, wd_d)):
                        P.add('sp', (lambda k=k, wdr=wdr, gi=gi: nc.sync.dma_start(
                            out=stg[k][:], in_=wdr[gi])), w=[K('stg%d' % k)], dma=True)
                        P.add('act', (lambda k=k, s=s: nc.scalar.copy(
                            out=wbf[s][k][:], in_=stg[k][:])),
                            r=[K('stg%d' % k)], w=[K('wbf%d_%d' % (s, k))])

                def gate_up(gi, sti):
                    s = gi % 2
                    t0, nn = sts[sti]
                    par = gu_cnt[0] % 2
                    gu_cnt[0] += 1
                    for g in range(G):
                        bG = g
                        bU = 2 + g
                        for k, bk in ((0, bG), (1, bU)):
                            for c in range(8):
                                P.add('pe', (lambda k=k, bk=bk, c=c, g=g, s=s, t0=t0, nn=nn: nc.tensor.matmul(
                                    banks[bk][:, :nn],
                                    wbf[s][k][:, g * 1024 + c * 128: g * 1024 + (c + 1) * 128],
                                    uT[:, c, t0:t0 + nn], start=(c == 0), stop=(c == 7))),
                                    r=[K('wbf%d_%d' % (s, k))] + ukeys(c, t0, nn), w=['bank%d' % bk])
                        P.add('act', (lambda bG=bG, g=g, nn=nn: nc.scalar.activation(
                            out=sg[:, g, :nn], in_=banks[bG][:, :nn], func=AF.Silu)),
                            r=['bank%d' % bG], w=[K('sg%d' % g)])
                        a = par * G + g
                        P.add('dve', (lambda a=a, g=g, bU=bU, nn=nn: nc.vector.tensor_tensor(
                            out=act[:, a, :nn], in0=sg[:, g, :nn], in1=banks[bU][:, :nn], op=ALU.mult)),
                            r=[K('sg%d' % g), 'bank%d' % bU], w=[K('act%d' % a)])
                    return par

                def down(gi, sti, par):
                    s = gi % 2
                    t0, nn = sts[sti]
                    for m in range(8):
                        bY = 4 + (m % 3)
                        for g in range(G):
                            a = par * G + g
                            P.add('pe', (lambda bY=bY, g=g, a=a, m=m, s=s, nn=nn: nc.tensor.matmul(
                                banks[bY][:, :nn],
                                wbf[s][2][:, g * 1024 + m * 128: g * 1024 + (m + 1) * 128],
                                act[:, a, :nn], start=(g == 0), stop=(g == G - 1))),
                                r=[K('wbf%d_2' % s), K('act%d' % a)], w=['bank%d' % bY])
                        P.add('dve', (lambda bY=bY, m=m, t0=t0, nn=nn: nc.vector.scalar_tensor_tensor(
                            out=hT[:, m, t0:t0 + nn], in0=banks[bY][:, :nn], scalar=0.5,
                            in1=hT[:, m, t0:t0 + nn], op0=ALU.mult, op1=ALU.add)),
                            r=['bank%d' % bY] + hkeys(m, t0, nn), w=hkeys(m, t0, nn))

                load_group(0)
                pend = None
                for gi in range(NG):
                    for sti in range(len(sts)):
                        par = gate_up(gi, sti)
                        if pend is not None:
                            down(*pend)
                        pend = (gi, sti, par)
                        if sti == 0 and gi + 1 < NG:
                            load_group(gi + 1)
                down(*pend)
                barrier(tag)

        bscr = sb("bscr", [128, 8], F32)
        bscr2 = sb("bscr2", [128, 8], F32)

        def tiny(e, col):
            if e == 'pe':
                return lambda: nc.tensor.matmul(banks[6][0:1, 0:1], ones[0:1, 0:1], ones[0:1, 0:1],
                                                start=True, stop=True)
            if e == 'act':
                return lambda: nc.scalar.copy(out=bscr[0:1, col:col + 1], in_=epsb[0:1, 0:1])
            if e == 'dve':
                return lambda: nc.vector.tensor_copy(out=bscr[0:1, col:col + 1], in_=epsb[0:1, 0:1])
            return lambda: nc.sync.dma_start(out=bscr2[0:1, col:col + 1], in_=epsb[0:1, 0:1])

        def barrier(tag):
            allk = ['__bar_' + e for e in ENGS]
            xr = {'pe': ['ones', 'bank6'], 'act': ['epsb'], 'dve': ['epsb'], 'sp': ['epsb']}
            xw = {'pe': ['bank6'], 'act': ['bscrA0'], 'dve': ['bscrD0'], 'sp': ['bscrS0']}
            xw2 = {'pe': ['bank6'], 'act': ['bscrA1'], 'dve': ['bscrD1'], 'sp': ['bscrS1']}
            col = {'pe': 0, 'act': 0, 'dve': 2, 'sp': 4}
            for e in ENGS:
                P.add(e, tiny(e, col[e]), r=xr[e], w=['__bar_' + e] + xw[e], dma=(e == 'sp'))
            for e in ENGS:
                P.add(e, tiny(e, col[e] + 1), r=allk + xr[e], w=xw2[e], dma=(e == 'sp'))

        SLOPES = [2.0 ** (-8.0 * (h + 1) / 8) for h in range(8)]
        cst_d = din("cst", [128, CW])
        win_d = din("win", [128, 8, 4368])
        wab_d = din("wab", [128, 8, 1024])
        wout_d = din("wo", [128, 8, 1024])
        cc_in = nc.dram_tensor("cc_in", [128, 516], F32)
        cc_out = nc.dram_tensor("cc_out", [N_CORES * 128, 516], F32)
        gw_d = din("gw", [17, 256])

        with ExitStack() as ns:
            sq = ns.enter_context(nc.sbuf_tensor("sq", [128, 2, 512], BF16))
            rs = ns.enter_context(nc.sbuf_tensor("rs", [128, 512], F32))
            cst = ns.enter_context(nc.sbuf_tensor("cstb", [128, CW], F32))
            P.add('sp', lambda: nc.sync.dma_start(out=cst[:], in_=cst_d), w=['cst'], dma=True)

            def C(name):
                o, wdt = COFF[name]
                return cst[:, o:o + wdt]

            with ExitStack() as p1:
                uT = p1.enter_context(nc.sbuf_tensor("uT1", [128, 8, T_ALL], BF16))
                if STAGE >= 1 and not os.environ.get('KSKIP1'):
                    norm_pass(supertiles(0, T_ALL), 0, sq, rs, "n1", uT)
                    ffn(supertiles(0, T_ALL), ffn_w_d[0], "f1", uT)

            if STAGE >= 3:
              with ExitStack() as pm:
                def msb(name, shape, dt):
                    return pm.enter_context(nc.sbuf_tensor("m_" + name, list(shape), dt))
                yaT = msb("yaT", [128, 4, T_OWN], BF16)
                ybT = msb("ybT", [128, 4, T_OWN], BF16)
                ident = msb("ident", [128, 128], BF16)
                gwb = msb("gwb", [16, 256], BF16)
                gbb = msb("gbb", [1, 256], BF16)
                gw32 = msb("gw32", [16, 256], F32)
                gb32 = msb("gb32", [1, 256], F32)
                P.add('dve', lambda: nc.vector.tensor_copy(out=ident[:], in_=C('ident')), r=['cst'], w=['ident'])
                P.add('sp', lambda: nc.sync.dma_start(out=gw32[:], in_=gw_d[0:16, :]), w=['gw32'], dma=True)
                P.add('sp', lambda: nc.sync.dma_start(out=gb32[:], in_=gw_d[16:17, :]), w=['gb32'], dma=True)
                P.add('dve', lambda: nc.vector.tensor_copy(out=gwb[:], in_=gw32[:]), r=['gw32'], w=['gwb'])
                P.add('dve', lambda: nc.vector.tensor_copy(out=gbb[:], in_=gb32[:]), r=['gb32'], w=['gbb'])
                rr = [0]

                def nb():
                    rr[0] = (rr[0] + 1) % 6
                    return rr[0]

                def BK(b):
                    return 'bank%s' % b

                stgw = msb("stgw", [128, 8, 128], F32)

                def load_cols(dst_fn, col0, ncols, key, src=None):
                    src = win_d if src is None else src
                    for c0 in range(0, ncols, 128):
                        w_ = min(128, ncols - c0)
                        P.add('sp', (lambda c0=c0, w_=w_: nc.sync.dma_start(
                            out=stgw[:, :, :w_], in_=src[:, :, col0 + c0:col0 + c0 + w_])),
                            w=['stgw'], dma=True)
                        for (dst, lo, hi) in dst_fn(c0, w_):
                            P.add('dve', (lambda dst=dst, lo=lo, hi=hi: nc.vector.tensor_copy(
                                out=dst, in_=stgw[:, :, lo:hi])), r=['stgw'], w=[key])

                pu = ExitStack()
                ust = pu.enter_context(nc.sbuf_tensor("m_ust", [128, 8, 512], BF16))

                def ust_keys(c, t0, nn):
                    return ['ust']

                if True:
                  with ExitStack() as pa:
                    def asb(name, shape, dt):
                        return pa.enter_context(nc.sbuf_tensor("a_" + name, list(shape), dt))
                    WQ = asb("WQ", [128, 8, 512], BF16)
                    WKd = asb("WKd", [128, 8, 2, 128], BF16)
                    WV = asb("WV", [128, 8, 128], BF16)
                    load_cols(lambda c0, w_: [(WQ[:, :, c0:c0 + w_], 0, w_)], 0, 512, 'WQ')
                    load_cols(lambda c0, w_: [(WKd[:, :, g, d0:d0 + 64], g * 64, g * 64 + 64)
                                              for g in range(2) for d0 in (0, 64)], 512, 128, 'WKd')
                    load_cols(lambda c0, w_: [(WV[:, :, :], 0, 128)], 640, 128, 'WV')
                    kbuf = asb("kbuf", [128, 4, 2, 128], BF16)
                    vbuf = asb("vbuf", [128, 4, 128], BF16)
                    qT = asb("qT", [128, 2, 4, 128], BF16)
                    Sb = asb("Sb", [128, 2, 272], F32)
                    Pb = asb("Pb", [128, 2, 272], BF16)
                    PT = asb("PT", [128, 2, 3, 128], BF16)
                    mx = asb("mx", [128, 8], F32)
                    negm = asb("negm", [128, 8], F32)
                    rsum = asb("rsum", [128, 8], F32)
                    es = asb("es", [128, 8], F32)
                    ya = asb("ya", [128, 512], BF16)
                    ust2 = asb("ust2", [128, 8, 512], BF16)
                    ustb = [ust, ust2]
                    rr4 = [0]

                    def nb4():
                        rr4[0] = (rr4[0] + 1) % 4
                        return rr4[0]

                    def kv_tile(U, uk, ul, slot):
                        for g in range(2):
                            b = nb4()
                            for c in range(8):
                                P.add('pe', (lambda b=b, g=g, c=c: nc.tensor.matmul(
                                    banks[b][:, 0:128], WKd[:, c, g, :], U[:, c, ul:ul + 128],
                                    start=(c == 0), stop=(c == 7))), r=['WKd', uk], w=[BK(b)])
                            P.add('act', (lambda b=b, g=g: nc.scalar.copy(
                                out=kbuf[:, slot, g, :], in_=banks[b][:, 0:128])),
                                r=[BK(b)], w=['kbuf%d' % slot])
                        b = nb4()
                        for c in range(8):
                            P.add('pe', (lambda b=b, c=c: nc.tensor.matmul(
                                banks[b][:, 0:128], U[:, c, ul:ul + 128], WV[:, c, :],
                                start=(c == 0), stop=(c == 7))), r=['WV', uk], w=[BK(b)])
                        P.add('act', (lambda b=b: nc.scalar.copy(
                            out=vbuf[:, slot, :], in_=banks[b][:, 0:128])), r=[BK(b)], w=['vbuf%d' % slot])

                    def q_tile(U, uk, ul, qs):
                        for ci in range(4):
                            b = nb4()
                            for c in range(8):
                                P.add('pe', (lambda b=b, ci=ci, c=c: nc.tensor.matmul(
                                    banks[b][:, 0:128], WQ[:, c, ci * 128:(ci + 1) * 128],
                                    U[:, c, ul:ul + 128], start=(c == 0), stop=(c == 7))),
                                    r=['WQ', uk], w=[BK(b)])
                            P.add('act', (lambda b=b, ci=ci: nc.scalar.mul(
                                out=qT[:, qs, ci, :], in_=banks[b][:, 0:128], mul=0.125)),
                                r=[BK(b)], w=['qT%d' % qs])

                    def head_A(h, ti, cur, prev, qs):
                        dist = C('dist1') if ti == 0 else C('dist')
                        g, ci, r0, q = h // 4, h // 2, (h % 2) * 64, h % 2
                        b = nb4()
                        for (slot, c0, w_) in ((prev, 0, 128), (cur, 128, 128), (3, 256, 16)):
                            P.add('pe', (lambda b=b, slot=slot, c0=c0, w_=w_, g=g, ci=ci, r0=r0: nc.tensor.matmul(
                                banks[b][:, c0:c0 + w_], qT[r0:r0 + 64, qs, ci, :],
                                kbuf[r0:r0 + 64, slot, g, 0:w_], start=True, stop=True)),
                                r=['qT%d' % qs, 'kbuf%d' % slot], w=[BK(b)])
                        P.add('dve', (lambda b=b, q=q, h=h, dist=dist: nc.vector.scalar_tensor_tensor(
                            out=Sb[:, q, :], in0=dist, scalar=-SLOPES[h], in1=banks[b][:, 0:272],
                            op0=ALU.mult, op1=ALU.add)), r=[BK(b), 'cst'], w=['Sb%d' % q])
                        P.add('dve', (lambda q=q, h=h: nc.vector.tensor_reduce(
                            out=mx[:, h:h + 1], in_=Sb[:, q, :], axis=AX.X, op=ALU.max)),
                            r=['Sb%d' % q], w=['mx%d' % h])
                        P.add('dve', (lambda h=h: nc.vector.tensor_scalar(
                            out=negm[:, h:h + 1], in0=mx[:, h:h + 1], scalar1=C('sinkb')[:, h:h + 1],
                            scalar2=-1.0, op0=ALU.max, op1=ALU.mult)), r=['mx%d' % h, 'cst'], w=['negm%d' % h])
                        P.add('act', (lambda q=q, h=h: nc.scalar.activation(
                            out=Pb[:, q, :], in_=Sb[:, q, :], func=AF.Exp, bias=negm[:, h:h + 1],
                            scale=1.0, accum_out=rsum[:, h:h + 1])),
                            r=['Sb%d' % q, 'negm%d' % h], w=['Pb%d' % q, 'rsum%d' % h])

                    def head_B(h, ti, cur, prev, bo):
                        g, q = h // 4, h % 2
                        pq = q * 512
                        for (blk, c0, w_) in ((0, 0, 128), (1, 128, 128), (2, 256, 16)):
                            P.add('pe', (lambda blk=blk, c0=c0, w_=w_, q=q, pq=pq: nc.tensor.transpose(
                                ptb[0:w_, pq + blk * 128:pq + (blk + 1) * 128], Pb[:, q, c0:c0 + w_], ident[:])),
                                r=['Pb%d' % q, 'ident'], w=['bankT%d' % q])
                        P.add('dve', (lambda q=q, pq=pq: nc.vector.tensor_copy(
                            out=PT[:, q, 0:2, :], in_=ptb[:, pq:pq + 256].rearrange("p (a b) -> p a b", a=2))),
                            r=['bankT%d' % q], w=['PT%d' % q])
                        P.add('dve', (lambda q=q, pq=pq: nc.vector.tensor_copy(
                            out=PT[0:16, q, 2, :], in_=ptb[0:16, pq + 256:pq + 384])), r=['bankT%d' % q], w=['PT%d' % q])
                        for (blk, slot, kk) in ((0, prev, 128), (1, cur, 128), (2, 3, 16)):
                            P.add('pe', (lambda bo=bo, blk=blk, slot=slot, kk=kk, q=q, g=g, h=h: nc.tensor.matmul(
                                banks[bo][:, h * 64:(h + 1) * 64], PT[0:kk, q, blk, :],
                                vbuf[0:kk, slot, g * 64:(g + 1) * 64], start=(blk == 0), stop=(blk == 2))),
                                r=['PT%d' % q, 'vbuf%d' % slot], w=[BK(bo)])

                    def att_finish(ti, bo):
                        allnegm = ['negm%d' % h for h in range(8)]
                        P.add('dve', lambda: nc.vector.tensor_tensor(
                            out=es[:], in0=C('sinkb'), in1=negm[:], op=ALU.add), r=allnegm + ['cst'], w=['es'])
                        P.add('act', lambda: nc.scalar.activation(out=es[:], in_=es[:], func=AF.Exp),
                              r=['es'], w=['es'])
                        P.add('dve', lambda: nc.vector.tensor_tensor(
                            out=es[:], in0=es[:], in1=rsum[:], op=ALU.add),
                            r=['es'] + ['rsum%d' % h for h in range(8)], w=['es'])
                        P.add('dve', lambda: nc.vector.reciprocal(out=es[:], in_=es[:]), r=['es'], w=['es'])
                        for h in range(8):
                            P.add('dve', (lambda bo=bo, h=h: nc.vector.tensor_scalar(
                                out=ya[:, h * 64:(h + 1) * 64], in0=banks[bo][:, h * 64:(h + 1) * 64],
                                scalar1=es[:, h:h + 1], scalar2=None, op0=ALU.mult)),
                                r=[BK(bo), 'es'], w=['ya'])
                        for k4 in range(4):
                            P.add('pe', (lambda k4=k4: nc.tensor.transpose(
                                ptb[:, k4 * 128:(k4 + 1) * 128], ya[:, k4 * 128:(k4 + 1) * 128], ident[:])),
                                r=['ya', 'ident'], w=['bankT0'])
                        P.add('act', (lambda ti=ti: nc.scalar.copy(
                            out=yaT[:, :, ti * 128:(ti + 1) * 128],
                            in_=ptb[:, 0:512].rearrange("p (a b) -> p a b", a=4))),
                            r=['bankT0'], w=['yaT%d' % ti])

                    own_sts_a = supertiles(OWN0, T_ALL)

                    def front(ti):
                        sti, j = divmod(ti, 4)
                        U = ustb[sti % 2]
                        uk = 'ust%d' % (sti % 2)
                        if j == 0:
                            t0, nn = own_sts_a[sti]
                            norm_pass([(t0, nn)], 1, sq, rs, "na", U, t0, (lambda c, a, b, uk=uk: [uk]))
                        kv_tile(U, uk, j * 128, ti % 3)
                        q_tile(U, uk, j * 128, ti % 2)

                    norm_pass([(0, 256)], 1, sq, rs, "na0", ust2, 0, (lambda c, a, b: ['ust1']))
                    kv_tile(ust2, 'ust1', 0, 3)
                    kv_tile(ust2, 'ust1', 128, 2)
                    front(0)
                    for ti in range(16):
                        cur, prev, qs = ti % 3, (ti - 1) % 3, ti % 2
                        bo = 4 + (ti % 2)
                        head_A(0, ti, cur, prev, qs)
                        for h in range(1, 8):
                            head_A(h, ti, cur, prev, qs)
                            if h == 4 and ti + 1 < 16:
                                front(ti + 1)
                            head_B(h - 1, ti, cur, prev, bo)
                        head_B(7, ti, cur, prev, bo)
                        att_finish(ti, bo)
                    barrier("att")

                if STAGE >= 4:
                  with ExitStack() as pg:
                    def gsb(name, shape, dt):
                        return pg.enter_context(nc.sbuf_tensor("g_" + name, list(shape), dt))
                    WGqd = gsb("WGqd", [128, 8, 4, 128], BF16)
                    WGkd = gsb("WGkd", [128, 8, 4, 128], BF16)
                    WGk = gsb("WGk", [128, 8, 256], BF16)
                    WGv = gsb("WGv", [128, 8, 512], BF16)
                    WGr = gsb("WGr", [128, 8, 512], BF16)
                    WGl = gsb("WGl", [128, 8, 16], BF16)

                    def dupdst(Wd):
                        def f(c0, w_):
                            h0 = c0 // 64
                            return [(Wd[:, :, h0 + hh, d0:d0 + 64], hh * 64, hh * 64 + 64)
                                    for hh in range(2) for d0 in (0, 64)]
                        return f
                    load_cols(dupdst(WGqd), 768, 256, 'WGqd')
                    load_cols(dupdst(WGkd), 1024, 256, 'WGkd')
                    load_cols(lambda c0, w_: [(WGk[:, :, c0:c0 + w_], 0, w_)], 1024, 256, 'WGk')
                    load_cols(lambda c0, w_: [(WGv[:, :, c0:c0 + w_], 0, w_)], 1280, 512, 'WGv')
                    load_cols(lambda c0, w_: [(WGr[:, :, c0:c0 + w_], 0, w_)], 1792, 512, 'WGr')
                    load_cols(lambda c0, w_: [(WGl[:, :, 0:16], 0, 16)], 2304, 16, 'WGl')
                    glT = gsb("glT", [16, 128], BF16)
                    ez = gsb("ez", [128, 256], F32)
                    la = gsb("la", [128, 256], F32)
                    lad = gsb("lad", [128, 4, 128], F32)
                    ee1 = gsb("ee1", [128, 256], F32)
                    dec = gsb("dec", [128, 4, 2], F32)
                    kdd = gsb("kdd", [128, 4, 128], BF16)
                    vg = gsb("vg", [128, 512], BF16)
                    S = gsb("S", [128, 512], F32)
                    Y = gsb("Y", [128, 512], BF16)
                    eb = gsb("eb", [128, 512], F32)
                    enb = gsb("enb", [128, 512], F32)
                    tq = gsb("tq", [128, 512], F32)
                    X = gsb("X", [128, 512], BF16)
                    keT = gsb("keT", [128, 512], BF16)
                    aTm = gsb("aTm", [128, 512], BF16)
                    sr = gsb("sr", [128, 512], F32)
                    ss = gsb("ss", [128, 4], F32)
                    yb = gsb("yb", [128, 512], BF16)
                    Lsave = gsb("Lsave", [128, 512], F32)
                    Ptot = gsb("Ptot", [128, 4], F32)
                    EX = gsb("EX", [128, 516], F32)
                    Gr = gsb("Gr", [128, 2, 516], F32)
                    Xs = gsb("Xs", [128, 512], F32)
                    Ep = gsb("Ep", [128, 512], F32)
                    Ap = gsb("Ap", [128, 4], F32)
                    P.add('dve', lambda: nc.vector.memset(S[:], 0.0), w=['S'])

                    def gla_tile(ul, ti, outputs, meta=False, ptot=False):
                        u_ = lambda c: ust[:, c, ul:ul + 128]
                        for c in range(8):
                            P.add('pe', (lambda c=c: nc.tensor.matmul(
                                banks[0][:, 0:256], u_(c), WGk[:, c, :], start=(c == 0), stop=(c == 7))),
                                r=['ust', 'WGk'], w=['bank0'])
                        for c in range(8):
                            P.add('pe', (lambda c=c: nc.tensor.matmul(
                                banks[1][:, 0:512], u_(c), WGv[:, c, :], start=(c == 0), stop=(c == 7))),
                                r=['ust', 'WGv'], w=['bank1'])
                        for c in range(8):
                            P.add('pe', (lambda c=c: nc.tensor.matmul(
                                banks[3][0:16, 256:384], WGl[:, c, :], u_(c), start=(c == 0), stop=(c == 7))),
                                r=['ust', 'WGl'], w=['bank3'])
                        P.add('act', lambda: nc.scalar.copy(out=glT[:], in_=banks[3][0:16, 256:384]),
                              r=['bank3'], w=['glT'])
                        P.add('pe', lambda: nc.tensor.matmul(
                            banks[0][:, 256:512], glT[:], gwb[:], start=True, stop=False),
                            r=['glT', 'gwb'], w=['bank0'])
                        P.add('pe', lambda: nc.tensor.matmul(
                            banks[0][:, 256:512], ones[0:1, 0:128], gbb[:], start=False, stop=True),
                            r=['ones', 'gbb'], w=['bank0'])
                        P.add('act', lambda: nc.scalar.activation(
                            out=ez[:], in_=banks[0][:, 256:512], func=AF.Exp, scale=-1.0),
                            r=['bank0'], w=['ez'])
                        P.add('act', lambda: nc.scalar.activation(
                            out=ez[:], in_=ez[:], func=AF.Ln, bias=C('one'), scale=1.0),
                            r=['ez', 'cst'], w=['ez'])
                        if meta:
                            P.add('dve', lambda: nc.vector.tensor_scalar(
                                out=la[:], in0=ez[:], scalar1=-1.0 / 16.0, scalar2=C('tokm'),
                                op0=ALU.mult, op1=ALU.mult), r=['ez', 'cst'], w=['la'])
                        else:
                            P.add('dve', lambda: nc.vector.tensor_scalar(
                                out=la[:], in0=ez[:], scalar1=-1.0 / 16.0, scalar2=None,
                                op0=ALU.mult), r=['ez'], w=['la'])
                        for d0 in (0, 64):
                            P.add('dve', (lambda d0=d0: nc.vector.tensor_copy(
                                out=lad[:, :, d0:d0 + 64], in_=la[:].rearrange("p (h d) -> p h d", h=4))),
                                r=['la'], w=['lad'])
                        for h in range(4):
                            P.add('pe', (lambda h=h: nc.tensor.matmul(
                                banks[2][:, h * 128:(h + 1) * 128], lad[:, h, :], C('U2'),
                                start=True, stop=True)), r=['lad', 'cst'], w=['bank2'])
                        P.add('pe', lambda: nc.tensor.matmul(
                            banks[3][:, 0:256], C('SU'), la[:], start=True, stop=True),
                            r=['la', 'cst'], w=['bank3'])
                        P.add('act', lambda: nc.scalar.activation(
                            out=ee1[:], in_=banks[3][:, 0:256], func=AF.Exp), r=['bank3'], w=['ee1'])
                        P.add('act', lambda: nc.scalar.activation(
                            out=dec[:], in_=banks[2][:, :].rearrange("p (h t) -> p h t", h=4)[:, :, 63::64],
                            func=AF.Exp), r=['bank2'], w=['dec'])
                        if outputs:
                            P.add('act', lambda: nc.scalar.activation(
                                out=eb[:], in_=banks[2][:, :], func=AF.Exp), r=['bank2'], w=['eb'])
                            P.add('act', lambda: nc.scalar.activation(
                                out=enb[:], in_=banks[2][:, :], func=AF.Exp, scale=-1.0),
                                r=['bank2'], w=['enb'])
                        for d0 in (0, 64):
                            P.add('dve', (lambda d0=d0: nc.vector.tensor_tensor(
                                out=kdd[:, :, d0:d0 + 64],
                                in0=banks[0][:, 0:256].rearrange("p (h d) -> p h d", h=4),
                                in1=ee1[:].rearrange("p (h d) -> p h d", h=4), op=ALU.mult)),
                                r=['bank0', 'ee1'], w=['kdd'])
                        P.add('act', lambda: nc.scalar.copy(out=vg[:], in_=banks[1][:, :]),
                              r=['bank1'], w=['vg'])
                        for (bk, lo) in ((4, 0), (5, 64)):
                            for h in range(4):
                                P.add('pe', (lambda bk=bk, lo=lo, h=h: nc.tensor.matmul(
                                    banks[bk][:, h * 128:(h + 1) * 128], kdd[lo:lo + 64, h, :],
                                    vg[lo:lo + 64, h * 128:(h + 1) * 128], start=True, stop=True)),
                                    r=['kdd', 'vg'], w=['bank%d' % bk])
                        if outputs:
                            P.add('dve', lambda: nc.vector.tensor_copy(out=Y[0:64, :], in_=S[0:64, :]),
                                  r=['S'], w=['Y'])
                        for h in range(4):
                            P.add('dve', (lambda h=h: nc.vector.scalar_tensor_tensor(
                                out=S[:, h * 128:(h + 1) * 128], in0=S[:, h * 128:(h + 1) * 128],
                                scalar=dec[:, h, 0:1], in1=banks[4][:, h * 128:(h + 1) * 128],
                                op0=ALU.mult, op1=ALU.add)), r=['S', 'dec', 'bank4'], w=['S'])
                        if outputs:
                            P.add('dve', lambda: nc.vector.tensor_copy(out=Y[64:128, :], in_=S[64:128, :]),
                                  r=['S'], w=['Y'])
                        for h in range(4):
                            P.add('dve', (lambda h=h: nc.vector.scalar_tensor_tensor(
                                out=S[:, h * 128:(h + 1) * 128], in0=S[:, h * 128:(h + 1) * 128],
                                scalar=dec[:, h, 1:2], in1=banks[5][:, h * 128:(h + 1) * 128],
                                op0=ALU.mult, op1=ALU.add)), r=['S', 'dec', 'bank5'], w=['S'])
                        if ptot:
                            for j in (0, 1):
                                P.add('dve', (lambda j=j: nc.vector.tensor_tensor(
                                    out=Ptot[:], in0=Ptot[:], in1=dec[:, :, j], op=ALU.mult)),
                                    r=['Ptot', 'dec'], w=['Ptot'])
                        if not outputs:
                            return
                        for (bk, Wd, key) in ((4, WGqd, 'WGqd'), (5, WGkd, 'WGkd')):
                            for h in range(4):
                                for c in range(8):
                                    P.add('pe', (lambda bk=bk, Wd=Wd, h=h, c=c: nc.tensor.matmul(
                                        banks[bk][:, h * 128:(h + 1) * 128], Wd[:, c, h, :], u_(c),
                                        start=(c == 0), stop=(c == 7))), r=['ust', key], w=['bank%d' % bk])
                        P.add('dve', lambda: nc.vector.scalar_tensor_tensor(
                            out=tq[:], in0=banks[4][:, :], scalar=0.125, in1=eb[:],
                            op0=ALU.mult, op1=ALU.mult), r=['bank4', 'eb'], w=['tq'])
                        P.add('dve', lambda: nc.vector.tensor_tensor(
                            out=X[:], in0=tq[:], in1=C('bmask'), op=ALU.mult), r=['tq', 'cst'], w=['X'])
                        P.add('dve', lambda: nc.vector.tensor_tensor(
                            out=keT[:], in0=banks[5][:, :], in1=enb[:], op=ALU.mult),
                            r=['bank5', 'enb'], w=['keT'])
                        for h in range(4):
                            P.add('pe', (lambda h=h: nc.tensor.matmul(
                                banks[0][:, h * 128:(h + 1) * 128], keT[:, h * 128:(h + 1) * 128],
                                X[:, h * 128:(h + 1) * 128], start=True, stop=True)),
                                r=['keT', 'X'], w=['bank0'])
                        P.add('dve', lambda: nc.vector.tensor_tensor(
                            out=aTm[:], in0=banks[0][:, :], in1=C('tmask'), op=ALU.mult),
                            r=['bank0', 'cst'], w=['aTm'])
                        for c in range(8):
                            P.add('pe', (lambda c=c: nc.tensor.matmul(
                                banks[1][:, 0:512], u_(c), WGr[:, c, :], start=(c == 0), stop=(c == 7))),
                                r=['ust', 'WGr'], w=['bank1'])
                        P.add('act', lambda: nc.scalar.activation(out=sr[:], in_=banks[1][:, :], func=AF.Silu),
                              r=['bank1'], w=['sr'])
                        for h in range(4):
                            hs = slice(h * 128, (h + 1) * 128)
                            P.add('pe', (lambda hs=hs: nc.tensor.matmul(
                                banks[1][:, hs], aTm[:, hs], vg[:, hs], start=True, stop=False)),
                                r=['aTm', 'vg'], w=['bank1'])
                            P.add('pe', (lambda hs=hs: nc.tensor.matmul(
                                banks[1][:, hs], X[:, hs], Y[:, hs], start=False, stop=True)),
                                r=['X', 'Y'], w=['bank1'])
                        P.add('act', lambda: nc.scalar.activation(out=tq[:], in_=banks[1][:, :], func=AF.Square),
                              r=['bank1'], w=['tq'])
                        P.add('dve', lambda: nc.vector.tensor_reduce(
                            out=ss[:], in_=tq[:].rearrange("p (h v) -> p h v", h=4), axis=AX.X, op=ALU.add),
                            r=['tq'], w=['ss'])
                        P.add('act', lambda: nc.scalar.activation(
                            out=ss[:], in_=ss[:], func=AF.Sqrt, bias=epsb[:, 0:1], scale=1.0 / 128.0),
                            r=['ss', 'epsb'], w=['ss'])
                        P.add('dve', lambda: nc.vector.reciprocal(out=ss[:], in_=ss[:]), r=['ss'], w=['ss'])
                        for h in range(4):
                            hs = slice(h * 128, (h + 1) * 128)
                            P.add('dve', (lambda hs=hs, h=h: nc.vector.scalar_tensor_tensor(
                                out=eb[:, hs], in0=banks[1][:, hs], scalar=ss[:, h:h + 1], in1=C('gn'),
                                op0=ALU.mult, op1=ALU.mult)), r=['bank1', 'ss', 'cst'], w=['eb'])
                        P.add('dve', lambda: nc.vector.tensor_tensor(
                            out=yb[:], in0=eb[:], in1=sr[:], op=ALU.mult), r=['eb', 'sr'], w=['yb'])
                        for k4 in range(4):
                            P.add('pe', (lambda k4=k4: nc.tensor.transpose(
                                ptb[:, k4 * 128:(k4 + 1) * 128], yb[:, k4 * 128:(k4 + 1) * 128], ident[:])),
                                r=['yb', 'ident'], w=['bankT'])
                        P.add('act', (lambda ti=ti: nc.scalar.copy(
                            out=ybT[:, :, ti * 128:(ti + 1) * 128],
                            in_=ptb[:, 0:512].rearrange("p (a b) -> p a b", a=4))),
                            r=['bankT'], w=['ybT%d' % ti])

                    norm_pass([(0, 128)], 1, sq, rs, "ng0", ust, 0, ust_keys)
                    gla_tile(0, -1, False, meta=True)
                    P.add('dve', lambda: nc.vector.tensor_scalar(
                        out=S[:], in0=S[:], scalar1=C('fmeta'), scalar2=None, op0=ALU.mult),
                        r=['S', 'cst'], w=['S'])
                    if not os.environ.get('KNOX'):
                        P.add('dve', lambda: nc.vector.tensor_copy(out=Lsave[:], in_=S[:]), r=['S'], w=['Lsave'])
                        P.add('dve', lambda: nc.vector.memset(Ptot[:], 1.0), w=['Ptot'])
                        for sti, (t0, nn) in enumerate(supertiles(OWN0, T_ALL)):
                            norm_pass([(t0, nn)], 1, sq, rs, "ng1", ust, t0, ust_keys)
                            for j in range(4):
                                gla_tile(j * 128, sti * 4 + j, False, ptot=True)
                        P.add('dve', lambda: nc.vector.tensor_copy(out=EX[:, 0:512], in_=S[:]), r=['S'], w=['EX'])
                        P.add('dve', lambda: nc.vector.tensor_copy(out=EX[:, 512:516], in_=Ptot[:]),
                              r=['Ptot'], w=['EX'])
                        P.add('sp', lambda: nc.sync.dma_start(out=cc_in.ap(), in_=EX[:]), r=['EX'], w=['cc_in'], dma=True)
                        P.add('pool', lambda: nc.gpsimd.collective_compute(
                            "AllGather", ALU.bypass, replica_groups=[list(range(N_CORES))],
                            ins=[cc_in.ap().opt()], outs=[cc_out.ap().opt()]),
                            r=['cc_in'], w=['cc_out'], dma='cc')
                        P.add('dve', lambda: nc.vector.memset(Xs[:], 0.0), w=['Xs'])
                        for r_ in range(N_CORES):
                            q = r_ % 2
                            P.add('sp', (lambda r_=r_, q=q: nc.sync.dma_start(
                                out=Gr[:, q, :], in_=cc_out.ap()[r_ * 128:(r_ + 1) * 128, :])),
                                r=['cc_out'], w=['Gr%d' % q], dma=True)
                            P.add('dve', (lambda r_=r_, q=q: nc.vector.tensor_scalar(
                                out=Ap[:], in0=Gr[:, q, 512:516], scalar1=C('sel')[:, r_:r_ + 1],
                                scalar2=C('nsel')[:, r_:r_ + 1], op0=ALU.mult, op1=ALU.add)),
                                r=['Gr%d' % q, 'cst'], w=['Ap'])
                            P.add('dve', (lambda r_=r_, q=q: nc.vector.tensor_scalar(
                                out=Ep[:], in0=Gr[:, q, 0:512], scalar1=C('sel')[:, r_:r_ + 1],
                                scalar2=None, op0=ALU.mult)), r=['Gr%d' % q, 'cst'], w=['Ep'])
                            for h in range(4):
                                hs = slice(h * 128, (h + 1) * 128)
                                P.add('dve', (lambda hs=hs, h=h: nc.vector.scalar_tensor_tensor(
                                    out=Xs[:, hs], in0=Xs[:, hs], scalar=Ap[:, h:h + 1], in1=Ep[:, hs],
                                    op0=ALU.mult, op1=ALU.add)), r=['Xs', 'Ap', 'Ep'], w=['Xs'])
                        P.add('dve', lambda: nc.vector.tensor_tensor(
                            out=S[:], in0=Lsave[:], in1=Xs[:], op=ALU.add), r=['Lsave', 'Xs'], w=['S'])
                    for sti, (t0, nn) in enumerate(supertiles(OWN0, T_ALL)):
                        norm_pass([(t0, nn)], 1, sq, rs, "ng", ust, t0, ust_keys)
                        for j in range(4):
                            gla_tile(j * 128, sti * 4 + j, True)
                    barrier("gla")
                if STAGE in (3, 4):
                    for k4 in range(4):
                        for (t0, nn) in supertiles(0, T_OWN):
                            P.add('dve', (lambda k4=k4, t0=t0, nn=nn: nc.vector.tensor_copy(
                                out=hT[:, k4, OWN0 + t0:OWN0 + t0 + nn], in_=yaT[:, k4, t0:t0 + nn])),
                                r=['yaT%d' % ti for ti in range(t0 // 128, (t0 + nn) // 128)],
                                w=hkeys(k4, OWN0 + t0, nn))
                    if STAGE == 4:
                        for k4 in range(4):
                            for (t0, nn) in supertiles(0, T_OWN):
                                P.add('dve', (lambda k4=k4, t0=t0, nn=nn: nc.vector.tensor_copy(
                                    out=hT[:, 4 + k4, OWN0 + t0:OWN0 + t0 + nn], in_=ybT[:, k4, t0:t0 + nn])),
                                    r=['ybT%d' % ti for ti in range(t0 // 128, (t0 + nn) // 128)],
                                    w=hkeys(4 + k4, OWN0 + t0, nn))

                pu.close()
                if STAGE >= 5:
                  with ExitStack() as pb:
                    def bsb(name, shape, dt):
                        return pb.enter_context(nc.sbuf_tensor("b_" + name, list(shape), dt))
                    uown = bsb("uown", [128, 8, T_OWN], BF16)
                    mixT = bsb("mixT", [128, 8, T_OWN], BF16)
                    wga = bsb("wga", [128, 1, 8, 128], BF16)
                    wgb = bsb("wgb", [128, 1, 8, 128], BF16)
                    wab = bsb("wab", [128, 1, 8, 128], BF16)
                    wo = bsb("wo", [128, 1, 8, 128], BF16)
                    sga = bsb("sga", [128, 2, 512], F32)
                    t1 = bsb("t1", [128, 512], F32)
                    t2 = bsb("t2", [128, 512], F32)
                    own_sts = supertiles(OWN0, T_ALL)

                    def ukeys_own(c, t0, nn):
                        return ['uo%d.%d' % (c, tt) for tt in range(t0 // 128, (t0 + nn) // 128)]
                    norm_pass(own_sts, 1, sq, rs, "nb", uown, OWN0, ukeys_own)

                    def mkeys(m, t0, nn):
                        return ['mx%d.%d' % (m, tt) for tt in range(t0 // 128, (t0 + nn) // 128)]
                    for m in range(8):
                        sl = 0
                        load_cols(lambda c0, w_, sl=sl: [(wga[:, sl, :, :], 0, 128)], 2320 + m * 128, 128, 'wga%d' % sl)
                        load_cols(lambda c0, w_, sl=sl: [(wgb[:, sl, :, :], 0, 128)], 3344 + m * 128, 128, 'wgb%d' % sl)
                        load_cols(lambda c0, w_, sl=sl: [(wab[:, sl, :, :], 0, 128)], m * 128, 128, 'wab%d' % sl, src=wab_d)
                        for (t0, nn) in own_sts:
                            o0 = t0 - OWN0
                            for br, (wg_, gk) in enumerate(((wga, 'wga%d' % sl), (wgb, 'wgb%d' % sl))):
                                yT, ykey = (yaT, 'yaT') if br == 0 else (ybT, 'ybT')
                                bA = nb()
                                for k4 in range(4):
                                    P.add('pe', (lambda bA=bA, k4=k4, br=br, yT=yT, o0=o0, nn=nn, sl=sl: nc.tensor.matmul(
                                        banks[bA][:, :nn], wab[:, sl, br * 4 + k4, :], yT[:, k4, o0:o0 + nn],
                                        start=(k4 == 0), stop=(k4 == 3))),
                                        r=['wab%d' % sl] + ['%s%d' % (ykey, tt) for tt in range(o0 // 128, (o0 + nn) // 128)],
                                        w=[BK(bA)])
                                bG = nb()
                                for c in range(8):
                                    P.add('pe', (lambda bG=bG, c=c, wg_=wg_, t0=t0, nn=nn, sl=sl: nc.tensor.matmul(
                                        banks[bG][:, :nn], wg_[:, sl, c, :], uown[:, c, t0 - OWN0:t0 - OWN0 + nn],
                                        start=(c == 0), stop=(c == 7))),
                                        r=[gk] + ukeys_own(c, t0, nn), w=[BK(bG)])
                                P.add('act', (lambda bG=bG, br=br, nn=nn: nc.scalar.activation(
                                    out=sga[:, br, :nn], in_=banks[bG][:, :nn], func=AF.Sigmoid)),
                                    r=[BK(bG)], w=['sga%d' % br])
                                tt_ = t1 if br == 0 else t2
                                P.add('dve', (lambda bA=bA, br=br, tt_=tt_, nn=nn: nc.vector.tensor_tensor(
                                    out=tt_[:, :nn], in0=sga[:, br, :nn], in1=banks[bA][:, :nn], op=ALU.mult)),
                                    r=['sga%d' % br, BK(bA)], w=['t%d' % (br + 1)])
                            P.add('dve', (lambda m=m, o0=o0, nn=nn: nc.vector.tensor_tensor(
                                out=mixT[:, m, o0:o0 + nn], in0=t1[:, :nn], in1=t2[:, :nn], op=ALU.add)),
                                r=['t1', 't2'], w=mkeys(m, o0, nn))
                    for m in range(8):
                        sl = 0
                        load_cols(lambda c0, w_, sl=sl: [(wo[:, sl, :, :], 0, 128)], m * 128, 128, 'wo%d' % sl, src=wout_d)
                        for (t0, nn) in own_sts:
                            o0 = t0 - OWN0
                            bO = nb()
                            for c in range(8):
                                P.add('pe', (lambda bO=bO, c=c, o0=o0, nn=nn, sl=sl: nc.tensor.matmul(
                                    banks[bO][:, :nn], wo[:, sl, c, :], mixT[:, c, o0:o0 + nn],
                                    start=(c == 0), stop=(c == 7))),
                                    r=['wo%d' % sl] + mkeys(c, o0, nn), w=[BK(bO)])
                            P.add('dve', (lambda bO=bO, m=m, t0=t0, nn=nn: nc.vector.tensor_tensor(
                                out=hT[:, m, t0:t0 + nn], in0=banks[bO][:, :nn], in1=hT[:, m, t0:t0 + nn],
                                op=ALU.add)), r=[BK(bO)] + hkeys(m, t0, nn), w=hkeys(m, t0, nn))
                barrier("mix")

            if STAGE >= 2 and not os.environ.get('KNOF2'):
                with ExitStack() as p2:
                    uT = p2.enter_context(nc.sbuf_tensor("uT2", [128, 8, T_ALL], BF16))
                    norm_pass(supertiles(OWN0, T_ALL), 2, sq, rs, "n2", uT)
                    ffn(supertiles(OWN0, T_ALL, split_last=True), ffn_w_d[1], "f2", uT)

            outb = ns.enter_context(nc.sbuf_tensor("outb", [128, 2, 8, 512], F32))
            oc = 0
            for (t0, nn) in supertiles(OWN0, T_ALL):
                bS = 6
                o = oc % 2
                oc += 1
                for c in range(8):
                    q = c % 2
                    P.add('act', (lambda c=c, q=q, t0=t0, nn=nn: nc.scalar.activation(
                        out=sq[:, q, :nn], in_=hT[:, c, t0:t0 + nn], func=AF.Square)),
                        r=hkeys(c, t0, nn), w=['sq%d' % q])
                    P.add('pe', (lambda c=c, q=q, nn=nn: nc.tensor.matmul(
                        banks[bS][:, :nn], ones[:], sq[:, q, :nn], start=(c == 0), stop=(c == 7))),
                        r=['ones', 'sq%d' % q], w=['bank%d' % bS])
                P.add('act', (lambda nn=nn: nc.scalar.activation(
                    out=rs[:, :nn], in_=banks[bS][:, :nn], func=AF.Sqrt, bias=epsb[:, 0:1],
                    scale=1.0 / D)), r=['bank%d' % bS, 'epsb'], w=['rs'])
                P.add('dve', (lambda nn=nn: nc.vector.reciprocal(
                    out=rs[:, :nn], in_=rs[:, :nn])), r=['rs'], w=['rs'])
                for c in range(8):
                    P.add('dve', (lambda c=c, o=o, t0=t0, nn=nn: nc.vector.scalar_tensor_tensor(
                        out=outb[:, o, c, :nn], in0=hT[:, c, t0:t0 + nn],
                        scalar=norms[:, 3, c:c + 1], in1=rs[:, :nn],
                        op0=ALU.mult, op1=ALU.mult)),
                        r=hkeys(c, t0, nn) + ['rs', 'norms'], w=['outb%d' % o])
                P.add('sp', (lambda o=o, t0=t0, nn=nn: nc.sync.dma_start(
                    out=out_d[:, :, t0 - OWN0:t0 - OWN0 + nn], in_=outb[:, o, :, :nn])),
                    r=['outb%d' % o], w=['OUT%d' % t0], dma=True)
            P.add('sp', lambda: nc.sync.nop(), r=['OUT%d' % t0 for (t0, nn) in supertiles(OWN0, T_ALL)])
            P.emit()
    return nc, P


def _fm(a):
    T = a.shape[0]
    return np.ascontiguousarray(a.reshape(T, 8, 128).transpose(2, 1, 0))


def _ffn_layout(wg, wu, wd):
    def gl(w):
        a = w.reshape(8, 128, NG, G, 128).transpose(2, 1, 3, 0, 4)
        return np.ascontiguousarray(a).reshape(NG, 128, G * 1024)
    d = wd.reshape(NG, G, 128, 1024).transpose(0, 2, 1, 3)
    return gl(wg), gl(wu), np.ascontiguousarray(d).reshape(NG, 128, G * 1024)


_CACHE = {}


def kernel(x, meta_tokens, ffn1_norm, ffn1_w_gate, ffn1_w_up, ffn1_w_down, mix_norm, w_in,
           gla_gate_w, gla_gate_b, gla_out_norm, att_sinks, w_branch_att, w_branch_gla, w_out,
           ffn2_norm, ffn2_w_gate, ffn2_w_up, ffn2_w_down, final_norm):
    f32 = np.float32
    x = np.asarray(x, f32)
    if 'nc' not in _CACHE:
        _CACHE['nc'] = build_nc()
    nc, P = _CACHE['nc']

    shared = {}
    nrm = np.stack([np.asarray(ffn1_norm, f32)[0], np.asarray(mix_norm, f32)[0],
                    np.asarray(ffn2_norm, f32)[0], np.asarray(final_norm, f32)], 0)
    shared["norms"] = np.ascontiguousarray(nrm.reshape(4, 8, 128).transpose(2, 0, 1))
    for l, (wg, wu, wd) in enumerate(((ffn1_w_gate, ffn1_w_up, ffn1_w_down),
                                      (ffn2_w_gate, ffn2_w_up, ffn2_w_down)), 1):
        a, b, c = _ffn_layout(np.asarray(wg, f32)[0], np.asarray(wu, f32)[0], np.asarray(wd, f32)[0])
        shared["f%d_wg" % l], shared["f%d_wu" % l], shared["f%d_wd" % l] = a, b, c

    W_in = np.asarray(w_in, f32)[0]
    shared["win"] = np.ascontiguousarray(W_in.reshape(8, 128, 4368).transpose(1, 0, 2))
    wa = np.asarray(w_branch_att, f32)[0].reshape(4, 128, 1024).transpose(1, 0, 2)
    wb = np.asarray(w_branch_gla, f32)[0].reshape(4, 128, 1024).transpose(1, 0, 2)
    shared["wab"] = np.ascontiguousarray(np.concatenate([wa, wb], 1))
    shared["wo"] = np.ascontiguousarray(np.asarray(w_out, f32)[0].reshape(8, 128, 1024).transpose(1, 0, 2))
    shared["gw"] = np.ascontiguousarray(np.concatenate(
        [np.asarray(gla_gate_w, f32)[0], np.asarray(gla_gate_b, f32)[0][None]], 0))
    BIG = 1.0e7
    p = np.arange(128)
    dist = np.zeros((128, 272), f32)
    kpos = np.arange(256) - 128
    dd = np.abs(p[:, None] - kpos[None, :]).astype(f32)
    qc = p[:, None] // 64
    kc = np.floor_divide(kpos[None, :], 64)
    valid = (kc <= qc) & (kc >= qc - 2)
    dist[:, :256] = np.where(valid, dd, BIG)
    dist1 = dist.copy()
    dist1[:, :128] = BIG
    same = (p[:, None] // 64) == (p[None, :] // 64)
    U2 = (same & (p[:, None] <= p[None, :])).astype(f32)
    SU = (same & (p[:, None] > p[None, :])).astype(f32)
    bm = ((p[:, None] // 64) == (p[None, :] // 64)).astype(f32)
    cst = np.zeros((128, CW), f32)

    def put(name, a):
        o, w_ = COFF[name]
        cst[:, o:o + w_] = a
    put('dist', dist)
    put('ident', np.eye(128, dtype=f32))
    put('U2', U2)
    put('SU', SU)
    put('bmask', np.tile(bm, (1, 4)))
    put('tmask', np.tile(U2, (1, 4)))
    put('sinkb', np.broadcast_to(np.asarray(att_sinks, f32)[0][None, :], (128, 8)))
    put('gn', np.broadcast_to(np.asarray(gla_out_norm, f32)[0][None, :], (128, 128)))
    put('tokm', (p < 16).astype(f32)[:, None])
    put('one', np.ones((128, 1), f32))

    meta = np.asarray(meta_tokens, f32)
    in_maps = []
    for core in range(N_CORES):
        b, s = divmod(core, 4)
        tok = np.zeros((T_ALL, D), f32)
        tok[0:16] = meta
        if s > 0:
            tok[128:256] = x[b, s * T_OWN - 128: s * T_OWN]
        tok[256:] = x[b, s * T_OWN:(s + 1) * T_OWN]
        m = dict(shared)
        m["xT"] = _fm(tok)
        cc = cst.copy()
        o, w_ = COFF['dist1']
        cc[:, o:o + w_] = dist1 if s == 0 else dist
        cc[:, COFF['fmeta'][0]] = 1.0 if s == 0 else 0.0
        for r in range(N_CORES):
            sv_ = 1.0 if (r // 4 == b and r < core) else 0.0
            cc[:, COFF['sel'][0] + r] = sv_
            cc[:, COFF['nsel'][0] + r] = 1.0 - sv_
        m["cst"] = cc
        in_maps.append(m)

    res = run_bass_kernel_spmd(nc, in_maps, core_ids=list(range(N_CORES)))
    out = np.empty((2, 8192, D), f32)
    for core in range(N_CORES):
        b, s = divmod(core, 4)
        oT = np.asarray(res.results[core]["outT"], f32)
        out[b, s * T_OWN:(s + 1) * T_OWN] = oT.transpose(2, 1, 0).reshape(T_OWN, D)
    return out
```
